# Optimizing a Trainium2 kernel written in Bass

```python
import jax, jax.numpy as jnp
from jax import lax
import numpy as np

D_MODEL = 1024
BATCH = 16
SEQ = 2048
DEPTH = 1

D_MIX = D_MODEL
W_POOL = D_MIX // 2
W_CONV = D_MIX - W_POOL
POOL_WINDOWS = (2, 4, 8, 16)
N_POOL_GROUPS = len(POOL_WINDOWS)
POOL_GC = W_POOL // N_POOL_GROUPS
CONV_WIDTH = 31
N_CONV_HEADS = 8
D_IN = W_POOL * 2 + W_CONV * 3
RMS_EPS = 1e-6
LN_EPS = 1e-5

kernel_name = "hybrid_pool_conformer_gated_layer"


def _rmsnorm(x, g):
    xf = x.astype(jnp.float32)
    y = xf * lax.rsqrt(jnp.mean(xf * xf, axis=-1, keepdims=True) + RMS_EPS)
    return (y * g.astype(jnp.float32)).astype(x.dtype)


def _layernorm(x, g, b):
    xf = x.astype(jnp.float32)
    mu = jnp.mean(xf, axis=-1, keepdims=True)
    var = jnp.mean(jnp.square(xf - mu), axis=-1, keepdims=True)
    y = (xf - mu) * lax.rsqrt(var + LN_EPS)
    return (y * g.astype(jnp.float32) + b.astype(jnp.float32)).astype(x.dtype)


def _causal_window_mean(u, window):
    S = u.shape[1]
    cs = jnp.cumsum(u.astype(jnp.float32), axis=1)
    cs = jnp.concatenate([jnp.zeros_like(cs[:, :1]), cs], axis=1)
    t = jnp.arange(S)
    hi = t + 1
    lo = jnp.maximum(hi - window, 0)
    sums = cs[:, hi] - cs[:, lo]
    cnt = (hi - lo).astype(jnp.float32)[None, :, None]
    return (sums / cnt).astype(u.dtype)


def _pool_mixer(u, pool_w, pool_b, pool_scale):
    B, S, _ = u.shape
    ug = u.reshape(B, S, N_POOL_GROUPS, POOL_GC)
    pooled = jnp.stack(
        [_causal_window_mean(ug[:, :, gi], w) for gi, w in enumerate(POOL_WINDOWS)],
        axis=2)
    d = pooled - ug
    z = jnp.einsum('bsgc,gcd->bsgd', d, pool_w) + pool_b
    return z.reshape(B, S, W_POOL) * pool_scale


def _conformer_conv(v, g, conv_dw, conv_b, ln_g, ln_b, pw_w, pw_b):
    h = v * jax.nn.sigmoid(g)
    C = h.shape[-1]
    h = lax.conv_general_dilated(
        h, conv_dw[:, None, :].astype(h.dtype),
        window_strides=(1,),
        padding=[(CONV_WIDTH - 1, 0)],
        dimension_numbers=('NWC', 'WIO', 'NWC'),
        feature_group_count=C) + conv_b
    h = _layernorm(h, ln_g, ln_b)
    h = jax.nn.silu(h)
    return jnp.einsum('bsc,cd->bsd', h, pw_w) + pw_b


def setup_inputs(seed: int = 0) -> dict:
    key = jax.random.key(seed)
    ks = jax.random.split(key, 16)
    f32 = jnp.float32
    x = jax.random.normal(ks[0], (BATCH, SEQ, D_MODEL), f32)
    norm_g = 1.0 + 0.02 * jax.random.normal(ks[1], (D_MODEL,), f32)
    w_in = jax.random.normal(ks[2], (D_MODEL, D_IN), f32) * D_MODEL ** -0.5
    pool_w = jax.random.normal(ks[3], (N_POOL_GROUPS, POOL_GC, POOL_GC), f32) * POOL_GC ** -0.5
    pool_b = 0.02 * jax.random.normal(ks[4], (N_POOL_GROUPS, POOL_GC), f32)
    pool_scale = 0.5 + 0.05 * jax.random.normal(ks[5], (W_POOL,), f32)
    conv_dw = jax.random.normal(ks[6], (CONV_WIDTH, W_CONV), f32) * CONV_WIDTH ** -0.5
    conv_b = 0.02 * jax.random.normal(ks[7], (W_CONV,), f32)
    ln_g = 1.0 + 0.02 * jax.random.normal(ks[8], (W_CONV,), f32)
    ln_b = 0.02 * jax.random.normal(ks[9], (W_CONV,), f32)
    pw_w = jax.random.normal(ks[10], (W_CONV, W_CONV), f32) * W_CONV ** -0.5
    pw_b = 0.02 * jax.random.normal(ks[11], (W_CONV,), f32)
    w_out = jax.random.normal(ks[12], (D_MIX, D_MODEL), f32) * D_MIX ** -0.5
    final_g = 1.0 + 0.02 * jax.random.normal(ks[13], (D_MODEL,), f32)
    return {"x": x, "norm_g": norm_g, "w_in": w_in,
            "pool_w": pool_w, "pool_b": pool_b, "pool_scale": pool_scale,
            "conv_dw": conv_dw, "conv_b": conv_b, "ln_g": ln_g, "ln_b": ln_b,
            "pw_w": pw_w, "pw_b": pw_b, "w_out": w_out, "final_g": final_g}


def reference(x, norm_g, w_in, pool_w, pool_b, pool_scale, conv_dw, conv_b,
              ln_g, ln_b, pw_w, pw_b, w_out, final_g):
    h = x
    for _ in range(DEPTH):
        hn = _rmsnorm(h, norm_g)
        proj = jnp.einsum('bsd,de->bse', hn, w_in)
        o = 0
        a_val = proj[..., o:o + W_POOL]; o += W_POOL
        a_gate = proj[..., o:o + W_POOL]; o += W_POOL
        b_val = proj[..., o:o + W_CONV]; o += W_CONV
        b_glu = proj[..., o:o + W_CONV]; o += W_CONV
        b_gate = proj[..., o:o + W_CONV]
        y_a = _pool_mixer(a_val, pool_w, pool_b, pool_scale) * jax.nn.silu(a_gate)
        y_b = _conformer_conv(b_val, b_glu, conv_dw, conv_b, ln_g, ln_b, pw_w, pw_b) * jax.nn.silu(b_gate)
        y = jnp.concatenate([y_a, y_b], axis=-1)
        h = h + jnp.einsum('bse,ed->bsd', y, w_out)
    return _rmsnorm(h, final_g)
```

```python
import contextlib
import numpy as np
import concourse.bass as bass
import concourse.mybir as mybir
from concourse.bass_utils import run_bass_kernel_spmd

F32 = mybir.dt.float32
BF16 = mybir.dt.bfloat16
I32 = mybir.dt.int32
AF = mybir.ActivationFunctionType
ALU = mybir.AluOpType

NCORES = 8
D = 1024
DIN = 2560
SEQ = 2048
TOK_PER_CORE = 2 * SEQ
T = 256
NT = TOK_PER_CORE // T
TPS = SEQ // T
KW = 31
POOL_W = (2, 4, 8, 16)
UH = 15
HH = 30
RMS_EPS = 1e-6
LN_EPS = 1e-5

C_NG, C_PS, C_PB, C_CB, C_LG, C_LB, C_PWB, C_CW = 0, 8, 12, 16, 20, 24, 28, 32
NGRP = 8
NCOLS = C_CW + 16 * NGRP
LS = T + 28


class Sem:
    def __init__(self, h, name):
        self.h = h
        self.name = name
        self.count = 0


class Buf:
    __slots__ = ("name", "writer", "readers")

    def __init__(self, name):
        self.name = name
        self.writer = None
        self.readers = {}


class Eng:
    def __init__(self, name, sem):
        self.name = name
        self.sem = sem
        self.ops = []
        self.waited = {}


class Sched:
    def __init__(self, nc, stack):
        self.nc = nc
        self.stack = stack
        self.engs = {}
        for n in ("pe", "act", "dve", "pool", "sp"):
            self.engs[n] = Eng(n, self.new_sem("e_" + n))
        self.bufs = {}

    def new_sem(self, name):
        return Sem(self.stack.enter_context(self.nc.semaphore(name)), name)

    def buf(self, name):
        b = self.bufs.get(name)
        if b is None:
            b = Buf(name)
            self.bufs[name] = b
        return b

    def emit(self, eng, fns, reads=(), writes=(), dma_sem=None):
        e = self.engs[eng]
        if not isinstance(fns, (list, tuple)):
            fns = [fns]
        waits = {}

        def need(tok):
            if tok is None:
                return
            sem, val = tok
            if eng == "pe" and sem is e.sem:
                return
            if e.waited.get(sem, 0) >= val:
                return
            if waits.get(sem, 0) < val:
                waits[sem] = val

        for b in reads:
            need(b.writer)
        for b in writes:
            need(b.writer)
            for s, v in b.readers.items():
                need((s, v))
        for s, v in waits.items():
            e.waited[s] = v
        if dma_sem is None:
            e.sem.count += 1
            tok = (e.sem, e.sem.count)
            amt = 1
        else:
            dma_sem.count += 16
            tok = (dma_sem, dma_sem.count)
            amt = 16
        e.ops.append((list(waits.items()), list(fns), tok[0], amt))
        for b in reads:
            if b.readers.get(tok[0], 0) < tok[1]:
                b.readers[tok[0]] = tok[1]
        for b in writes:
            b.writer = tok
            b.readers = {}
        return tok

    def wait_only(self, eng, toks):
        e = self.engs[eng]
        waits = {}
        for sem, val in toks:
            if e.waited.get(sem, 0) >= val:
                continue
            if waits.get(sem, 0) < val:
                waits[sem] = val
        for s, v in waits.items():
            e.waited[s] = v
        e.ops.append((list(waits.items()), [], None, 0))

    def replay(self, eng, handle):
        for waits, fns, sem, amt in self.engs[eng].ops:
            for s, v in waits:
                handle.wait_ge(s.h, v)
            ins = None
            for f in fns:
                ins = f(handle)
            if ins is not None and sem is not None:
                ins.then_inc(sem.h, amt)


def build_nc(debug=None):
    nc = bass.Bass("TRN2", target_bir_lowering=False)
    dram = {}

    def din(name, shape):
        dram[name] = nc.dram_tensor(name, shape, F32, kind="ExternalInput")
        return dram[name]

    x_d = din("x", [TOK_PER_CORE, D])
    win_d = din("w_in", [D, DIN])
    wout_d = din("w_out", [D, D])
    pww_d = din("pw_w", [128, 4 * 512])
    plw_d = din("pool_w", [128, 4 * 128])
    cols_d = din("cols", [128, NCOLS])
    gfin_d = din("gfin", [128, D])
    out_d = nc.dram_tensor("out", [TOK_PER_CORE, D], F32, kind="ExternalOutput")
    x_t = x_d.ap().rearrange("(n j p) d -> n p j d", p=128, j=2)
    out_t = out_d.ap().rearrange("(n j p) d -> n p j d", p=128, j=2)

    dbg_out = {}

    with contextlib.ExitStack() as st:
        S = Sched(nc, st)

        def sb(name, shape, dt):
            return st.enter_context(nc.sbuf_tensor("sb_" + name, shape, dt))

        cols = sb("cols", [128, NCOLS], F32)
        gfin = sb("gfin", [128, D], F32)
        ident_f = sb("ident_f", [128, 128], F32)
        ident_bf = sb("ident_bf", [128, 128], BF16)
        ones_bf = sb("ones_bf", [128, 128], BF16)
        identblk = sb("identblk", [128, 32], F32)
        cwh = sb("cwh", [128, 16 * NGRP], F32)
        Wd = sb("Wd", [128, 16 * NGRP, 32], BF16)
        hs = sb("hs", [128, 16, HH + T + 2], BF16)
        fac = sb("fac", [128, 4, 16], F32)
        neghalf = sb("neghalf", [128, 8], F32)
        w_in_bf = sb("w_in_bf", [128, 8, DIN], BF16)
        w_out_bf = sb("w_out_bf", [128, 8, D], BF16)
        pw_bf = sb("pw_bf", [128, 4, 512], BF16)
        pwa_bf = sb("pwa_bf", [128, 4, 128], BF16)
        NXB = 5
        xb = [sb(f"xb{i}", [128, 2, D], F32) for i in range(NXB)]
        xn = [sb(f"xn{i}", [128, 2, D], BF16) for i in range(2)]
        xT = [sb(f"xT{i}", [128, 8, T], BF16) for i in range(2)]
        ub = [sb(f"ub{i}", [128, 4, UH + T + 1], F32) for i in range(2)]
        sA = sb("sA", [128, UH + T + 1], F32)
        sB = sb("sB", [128, UH + T + 1], F32)
        sbf = sb("sbf", [128, 4, T], BF16)
        sS = sb("sS", [128, 4, T], F32)
        ga = sb("ga", [128, 4, T], BF16)
        gb = [sb(f"gb{i}", [128, 4, T], BF16) for i in range(2)]
        th = sb("th", [128, 4, T], F32)
        hb = [sb(f"hb{i}", [128, 4, HH + T + 2], BF16) for i in range(2)]
        c32 = sb("c32", [128, 4, T], F32)
        cbf = sb("cbf", [128, 4, T], BF16)
        csq = sb("csq", [128, 4, T], BF16)
        mu_sb = sb("mu_sb", [128, T], F32)
        t1 = sb("t1", [128, T], F32)
        vareps = sb("vareps", [128, T], F32)
        rstd_ln = sb("rstd_ln", [128, T], F32)
        nmr = sb("nmr", [128, T], F32)
        s_bf = sb("s_bf", [128, 4, T], BF16)
        yT = [sb(f"yT{i}", [128, 8, T], BF16) for i in range(2)]
        ssq = sb("ssq", [128, 2 * NT], F32)
        msx = sb("msx", [128, 2 * NT], F32)
        rstdx = sb("rstdx", [128, 2 * NT], F32)
        ssq2 = sb("ssq2", [128, 2 * NT], F32)
        ms2 = sb("ms2", [128, 2 * NT], F32)
        rstd2 = sb("rstd2", [128, 2 * NT], F32)
        junk = sb("junk", [128, D], BF16)

        banks = [st.enter_context(nc.psum_tensor(f"bank{i}", [128, 512], F32)) for i in range(8)]
        bankB = [S.buf(f"bank{i}") for i in range(8)]
        gen_state = {"next": 0}
        GEN_BANKS = (4, 5, 6, 7)

        def next_bank():
            b = GEN_BANKS[gen_state["next"] % len(GEN_BANKS)]
            gen_state["next"] += 1
            return b

        B = S.buf
        sem_small = S.new_sem("d_small")
        sem_small2 = S.new_sem("d_small2")
        sem_hs = S.new_sem("d_hs")
        sem_xl = [S.new_sem(f"d_xl{i}") for i in range(NXB)]
        sem_xs = [S.new_sem(f"d_xs{i}") for i in range(NXB)]
        sem_dbg = S.new_sem("d_dbg")

        def xbufs(slot):
            return [B(f"xb{slot}_{j}_{h}") for j in range(2) for h in range(2)]

        def dbg(name, tensor, shape, dt, reads):
            if debug is None or name not in debug:
                return
            dd = nc.dram_tensor("dbg_" + name, shape, dt, kind="ExternalOutput")
            dbg_out[name] = dd
            S.emit("sp", lambda e: e.dma_start(out=dd.ap(), in_=tensor), reads=reads, dma_sem=sem_dbg)

        S.emit("sp", lambda e: e.dma_start(out=cols[:, :], in_=cols_d.ap()), writes=[B("cols")], dma_sem=sem_small)
        def emit_gfin():
            S.emit("sp", lambda e: e.dma_start(out=gfin[:, :], in_=gfin_d.ap()), writes=[B("gfin")], dma_sem=sem_small2)

        S.emit("pool", lambda e: e.memset(ident_f[:, :], 0.0), writes=[B("ident_f")])
        S.emit("pool", lambda e: e.affine_select(out=ident_f[:, :], in_=ident_f[:, :], pattern=[[-1, 128]],
                                                  compare_op=ALU.not_equal, fill=1.0, base=0, channel_multiplier=1),
               reads=[B("ident_f")], writes=[B("ident_f")])
        S.emit("pool", lambda e: e.tensor_copy(out=ident_bf[:, :], in_=ident_f[:, :]), reads=[B("ident_f")], writes=[B("ident_bf")])
        S.emit("pool", lambda e: e.memset(ones_bf[:, :], 1.0 / 512.0), writes=[B("ones_bf")])
        S.emit("pool", lambda e: e.memset(neghalf[:, :], -0.5), writes=[B("neghalf")])
        S.emit("pool", lambda e: e.tensor_tensor(out=identblk[:, :], in0=ident_f[:, 0:32], in1=ident_f[:, 32:64], op=ALU.add),
               reads=[B("ident_f")], writes=[B("identblk")])
        S.emit("pool", lambda e: e.tensor_tensor(out=identblk[:, :], in0=identblk[:, :], in1=ident_f[:, 64:96], op=ALU.add),
               reads=[B("ident_f"), B("identblk")], writes=[B("identblk")])
        S.emit("pool", lambda e: e.tensor_tensor(out=identblk[:, :], in0=identblk[:, :], in1=ident_f[:, 96:128], op=ALU.add),
               reads=[B("ident_f"), B("identblk")], writes=[B("identblk")])
        for par_ in range(2):
            S.emit("pool", (lambda par_=par_: (lambda e: e.memset(hb[par_][:, :, HH + T:HH + T + 2], 0.0)))(), writes=[B(f"hpad{par_}")])
        S.emit("pool", lambda e: e.memset(fac[:, :, :], 1.0), writes=[B("fac")])
        for g, w in enumerate(POOL_W):
            for t in range(w - 1):
                S.emit("pool", (lambda g=g, t=t, w=w: (lambda e: e.memset(fac[:, g, t:t + 1], float(w) / float(t + 1))))(),
                       reads=[B("fac")], writes=[B("fac")])
        S.emit("pool", lambda e: e.tensor_scalar(out=cwh[:, :], in0=cols[:, C_CW:C_CW + 16 * NGRP], scalar1=0.5, scalar2=0.0, op0=ALU.mult, op1=ALU.add),
               reads=[B("cols")], writes=[B("cwh")])
        ib_ap = identblk[:, :]
        cw_ap = cwh[:, :]
        ib_b = bass.AP(identblk, 0, [list(ib_ap.ap[0]), [0, 16 * NGRP], [1, 32]])
        cw_b = bass.AP(cwh, 0, [list(cw_ap.ap[0]), [1, 16 * NGRP], [0, 32]])
        S.emit("pool", lambda e: e.tensor_tensor(out=Wd[:, :, :], in0=ib_b, in1=cw_b, op=ALU.mult),
               reads=[B("identblk"), B("cwh")], writes=[B("Wd")])

        stage_slots = [3, 4]
        stage_sem = {2: S.new_sem("d_st2"), 3: S.new_sem("d_st3"), 4: S.new_sem("d_st4")}
        stage_i = [0]
        cast_i = [0]

        wide_stage = [True]

        def next_stage():
            lst = (2, 3, 4) if wide_stage[0] else (3, 4)
            slot = lst[stage_i[0] % len(lst)]
            stage_i[0] += 1
            return slot

        def cast_op(dst_ap, src_ap, scale, rd, wr, eng=None):
            if eng is None:
                eng = ("dve", "act", "dve")[cast_i[0] % 3]
                cast_i[0] += 1
            if eng == "act":
                if scale is None:
                    fn = lambda e: e.activation(out=dst_ap, in_=src_ap, func=AF.Copy)
                else:
                    fn = lambda e: e.activation(out=dst_ap, in_=src_ap, func=AF.Copy, scale=scale)
            else:
                if scale is None:
                    fn = lambda e: e.tensor_copy(out=dst_ap, in_=src_ap)
                else:
                    fn = lambda e: e.tensor_scalar(out=dst_ap, in0=src_ap, scalar1=scale, scalar2=None, op0=ALU.mult)
            S.emit(eng, fn, reads=rd, writes=wr)

        winC_slot = {}

        def emit_win(blk, hook3=None):
            pst = list(xb[0][:, :, :].ap[0])
            if blk in ("A", "B"):
                c0 = 1024 if blk == "A" else 0
                blks = (2, 3) if blk == "A" else (0, 1)
                for pr in range(4):
                    if pr == 3 and hook3 is not None:
                        hook3()
                    slot = next_stage()
                    sv = xb[slot][:, :, :]
                    src = win_d.ap()[pr * 256:(pr + 1) * 256, c0:c0 + 1024].rearrange("(c p) n -> p c n", p=128)
                    S.emit("sp", (lambda sv=sv, src=src: (lambda e: e.dma_start(out=sv, in_=src)))(), writes=xbufs(slot), dma_sem=stage_sem[slot])
                    for cl in range(2):
                        c = 2 * pr + cl
                        cast_op(w_in_bf[:, c, c0:c0 + 1024], xb[slot][:, cl, :], cols[:, C_NG + c:C_NG + c + 1],
                                xbufs(slot) + [B("cols")], [B(f"win{blks[0]}_{pr // 2}"), B(f"win{blks[1]}_{pr // 2}")])
            elif blk == "C_dma":
                for half in range(2):
                    slot = next_stage()
                    winC_slot[half] = slot
                    sv = bass.AP(xb[slot], 0, [pst, [512, 4], [1, 512]])
                    src = win_d.ap()[half * 512:(half + 1) * 512, 2048:2560].rearrange("(c p) n -> p c n", p=128)
                    S.emit("sp", (lambda sv=sv, src=src: (lambda e: e.dma_start(out=sv, in_=src)))(), writes=xbufs(slot), dma_sem=stage_sem[slot])
            else:
                for half in range(2):
                    slot = winC_slot[half]
                    sv = bass.AP(xb[slot], 0, [pst, [512, 4], [1, 512]])
                    for cl in range(4):
                        c = half * 4 + cl
                        cast_op(w_in_bf[:, c, 2048:2560], sv[:, cl, :], cols[:, C_NG + c:C_NG + c + 1],
                                xbufs(slot) + [B("cols")], [B(f"win4_{half}")])

        def emit_poolpw_dma(slot_pl, slot_pw):
            pst = list(xb[0][:, :, :].ap[0])
            sv_pl = bass.AP(xb[slot_pl], 0, [pst, [1, 512]])
            S.emit("sp", lambda e: e.dma_start(out=sv_pl, in_=plw_d.ap()), writes=xbufs(slot_pl), dma_sem=stage_sem[slot_pl])
            sv = bass.AP(xb[slot_pw], 0, [pst, [1, 2048]])
            S.emit("sp", (lambda sv=sv: (lambda e: e.dma_start(out=sv, in_=pww_d.ap())))(), writes=xbufs(slot_pw), dma_sem=stage_sem[slot_pw])

        def emit_poolpw_cast(slot_pl, slot_pw):
            pst = list(xb[0][:, :, :].ap[0])
            sv_pl = bass.AP(xb[slot_pl], 0, [pst, [1, 512]])
            for g, w in enumerate(POOL_W):
                S.emit("dve", (lambda g=g, w=w: (lambda e: e.tensor_scalar(out=pwa_bf[:, g, :], in0=sv_pl[:, g * 128:(g + 1) * 128],
                                                                          scalar1=1.0 / w, scalar2=None, op0=ALU.mult)))(),
                       reads=xbufs(slot_pl), writes=[B("wts")])
            sv = bass.AP(xb[slot_pw], 0, [pst, [1, 2048]])
            pw_flat = bass.AP(pw_bf, 0, [list(pw_bf[:, :, :].ap[0]), [1, 2048]])
            for hf in range(2):
                cast_op(pw_flat[:, hf * 1024:(hf + 1) * 1024], sv[:, hf * 1024:(hf + 1) * 1024], None, xbufs(slot_pw), [B("pww")],
                        eng="dve")

        wout_slot = {}

        def emit_wout_dma(m, slot, after=()):
            wout_slot[m] = slot
            sv = xb[slot][:, :, :]
            src = wout_d.ap()[m * 256:(m + 1) * 256, :].rearrange("(c p) n -> p c n", p=128)
            S.emit("sp", (lambda sv=sv, src=src: (lambda e: e.dma_start(out=sv, in_=src)))(), reads=list(after), writes=xbufs(slot), dma_sem=stage_sem[slot])

        def emit_wout_cast(m, eng=None):
            slot = wout_slot[m]
            for cl in range(2):
                c = 2 * m + cl
                sc = cols[:, C_PS + c:C_PS + c + 1] if c < 4 else None
                cast_op(w_out_bf[:, c, :], xb[slot][:, cl, :], sc, xbufs(slot) + [B("cols")], [B("wout")], eng=eng)

        def rsqrt_small(src_ssq, ms_t, out_t, c0, eps, scale, nsrc, nms, nout):
            S.emit("pool", lambda e: e.tensor_scalar(out=ms_t[:, c0:c0 + 2], in0=src_ssq[:, c0:c0 + 2], scalar1=scale, scalar2=eps,
                                                      op0=ALU.mult, op1=ALU.add),
                   reads=[B(f"{nsrc}{c0}")], writes=[B(f"{nms}{c0}")])
            S.emit("pool", lambda e: e.tensor_tensor(out=out_t[:, c0:c0 + 2], in0=ms_t[:, c0:c0 + 2], in1=neghalf[:, 0:2], op=ALU.pow),
                   reads=[B(f"{nms}{c0}"), B("neghalf")], writes=[B(f"{nout}{c0}")])

        def stage_Aload(i):
            slot = i % NXB
            S.emit("sp", lambda e: e.dma_start(out=xb[slot][:, :, :], in_=x_t[i]), writes=xbufs(slot), dma_sem=sem_xl[slot])

        def stage_Acomp_a(i):
            slot = i % NXB
            par = i % 2
            for j in range(2):
                S.emit("act", (lambda j=j: (lambda e: e.activation(out=xn[par][:, j, :], in_=xb[slot][:, j, :], func=AF.Square,
                                                                  accum_out=ssq[:, 2 * i + j:2 * i + j + 1])))(),
                       reads=[B(f"xb{slot}_{j}_0"), B(f"xb{slot}_{j}_1")], writes=[B(f"xn{par}_{j}"), B(f"ssq{2 * i}")])
            rsqrt_small(ssq, msx, rstdx, 2 * i, RMS_EPS, 1.0 / D, "ssq", "msx", "rstdx")

        def stage_Acomp_b(i):
            slot = i % NXB
            par = i % 2
            for j in range(2):
                S.emit("dve", (lambda j=j: (lambda e: e.tensor_scalar(out=xn[par][:, j, :], in0=xb[slot][:, j, :],
                                                                     scalar1=rstdx[:, 2 * i + j:2 * i + j + 1], scalar2=None, op0=ALU.mult)))(),
                       reads=[B(f"xb{slot}_{j}_0"), B(f"xb{slot}_{j}_1"), B(f"rstdx{2 * i}")], writes=[B(f"xn{par}_{j}")])

        def stage_Acomp(i):
            stage_Acomp_a(i)
            stage_Acomp_b(i)

        def stage_B(i):
            par = i % 2
            for grp in range(2):
                bk = 2 + grp
                pbf = banks[bk].bitcast(BF16)
                fns = []
                for dcl in range(4):
                    dc = grp * 4 + dcl
                    for j in range(2):
                        fns.append((lambda dc=dc, dcl=dcl, j=j, pbf=pbf: (lambda e: e.transpose(
                            out=pbf[:, dcl * T + j * 128: dcl * T + (j + 1) * 128],
                            in_=xn[par][:, j, dc * 128:(dc + 1) * 128], identity=ident_bf[:, :])))())
                S.emit("pe", fns, reads=[B(f"xn{par}_0"), B(f"xn{par}_1"), B("ident_bf")], writes=[bankB[bk]])
                dst = bass.AP(xT[par], grp * 4 * T, [list(xT[par][:, :, :].ap[0]), [1, 4 * T]])
                S.emit("act", (lambda pbf=pbf, dst=dst: (lambda e: e.activation(out=dst, in_=pbf[:, 0:4 * T], func=AF.Copy)))(),
                       reads=[bankB[bk]], writes=[B(f"xT{par}_{grp}")])

        def proj_group(i, f):
            par = i % 2
            bk = next_bank()
            fns = []
            for dc in range(8):
                fns.append((lambda dc=dc, bk=bk: (lambda e: e.matmul(out=banks[bk][:, 0:T], lhsT=w_in_bf[:, dc, f * 128:(f + 1) * 128],
                                                                   rhs=xT[par][:, dc, :], start=(dc == 0), stop=(dc == 7))))())
            S.emit("pe", fns, reads=[B(f"win{f // 4}_0"), B(f"win{f // 4}_1"), B(f"xT{par}_0"), B(f"xT{par}_1")], writes=[bankB[bk]])
            return bk

        def stage_C(i, which, qs=(0, 1, 2, 3)):
            par = i % 2
            first = (i % TPS == 0)
            for kind in which:
                for q in qs:
                    if kind == "glu":
                        bk = proj_group(i, 12 + q)
                        S.emit("act", (lambda q=q, bk=bk: (lambda e: e.activation(out=th[:, q, :], in_=banks[bk][:, 0:T], func=AF.Tanh, scale=0.5)))(),
                               reads=[bankB[bk]], writes=[B(f"th{q}")])
                    elif kind == "bval":
                        if q == 0:
                            for qq in range(4):
                                if first:
                                    S.emit("pool", (lambda qq=qq: (lambda e: e.memset(hb[par][:, qq, 0:HH], 0.0)))(),
                                           writes=[B(f"hh{par}_{qq}")])
                                else:
                                    S.emit("pool", (lambda qq=qq: (lambda e: e.tensor_copy(out=hb[par][:, qq, 0:HH], in_=hb[1 - par][:, qq, T:T + HH])))(),
                                           reads=[B(f"hm{1 - par}_{qq}")], writes=[B(f"hh{par}_{qq}")])
                        bk = proj_group(i, 8 + q)
                        S.emit("dve", (lambda q=q, bk=bk: (lambda e: e.scalar_tensor_tensor(out=hb[par][:, q, HH:HH + T], in0=th[:, q, :], scalar=1.0,
                                                                                          in1=banks[bk][:, 0:T], op0=ALU.add, op1=ALU.mult)))(),
                               reads=[bankB[bk], B(f"th{q}")], writes=[B(f"hm{par}_{q}")])
                    elif kind == "aval":
                        if q == 0:
                            for gg in range(4):
                                if first:
                                    S.emit("pool", (lambda gg=gg: (lambda e: e.memset(ub[par][:, gg, 0:UH], 0.0)))(),
                                           writes=[B(f"uh{par}_{gg}")])
                                else:
                                    S.emit("pool", (lambda gg=gg: (lambda e: e.tensor_copy(out=ub[par][:, gg, 0:UH], in_=ub[1 - par][:, gg, T:T + UH])))(),
                                           reads=[B(f"um{1 - par}_{gg}")], writes=[B(f"uh{par}_{gg}")])
                        bk = proj_group(i, 0 + q)
                        S.emit("act", (lambda q=q, bk=bk: (lambda e: e.activation(out=ub[par][:, q, UH:UH + T], in_=banks[bk][:, 0:T], func=AF.Copy)))(),
                               reads=[bankB[bk]], writes=[B(f"um{par}_{q}")])
                    elif kind == "agate":
                        bk = proj_group(i, 4 + q)
                        S.emit("act", (lambda q=q, bk=bk: (lambda e: e.activation(out=ga[:, q, :], in_=banks[bk][:, 0:T], func=AF.Silu)))(),
                               reads=[bankB[bk]], writes=[B(f"ga{q}")])
                    elif kind == "bgate":
                        bk = proj_group(i, 16 + q)
                        S.emit("act", (lambda q=q, bk=bk: (lambda e: e.activation(out=gb[par][:, q, :], in_=banks[bk][:, 0:T], func=AF.Silu)))(),
                               reads=[bankB[bk]], writes=[B(f"gb{par}_{q}")])

        def stage_Dpool(i):
            par = i % 2
            first = (i % TPS == 0)
            for g, w in enumerate(POOL_W):
                nlev = g + 1
                src = ub[par]
                rd = [B(f"um{par}_{g}"), B(f"uh{par}_{g}")]
                cur_lo = 0
                cur = None
                for l in range(nlev):
                    sh = 1 << l
                    lo = cur_lo + sh
                    last = (l == nlev - 1)
                    if last:
                        lo = UH
                    n = UH + T - lo
                    if cur is None:
                        in_a = src[:, g, lo:lo + n]
                        in_b = src[:, g, lo - sh:lo - sh + n]
                        rds = rd
                    else:
                        in_a = cur[:, lo:lo + n]
                        in_b = cur[:, lo - sh:lo - sh + n]
                        rds = [B("sA" if cur is sA else "sB")]
                    if last:
                        out_ap = sS[:, g, :]
                        wr = [B(f"sS{g}")]
                        nxt = None
                    else:
                        nxt = sA if (cur is not sA) else sB
                        out_ap = nxt[:, lo:lo + n]
                        wr = [B("sA" if nxt is sA else "sB")]
                    S.emit("pool", (lambda out_ap=out_ap, in_a=in_a, in_b=in_b: (lambda e: e.tensor_tensor(out=out_ap, in0=in_a, in1=in_b, op=ALU.add)))(),
                           reads=rds, writes=wr)
                    cur = nxt
                    cur_lo = lo
                if first:
                    S.emit("pool", (lambda g=g, w=w: (lambda e: e.tensor_tensor(out=sS[:, g, 0:w - 1], in0=sS[:, g, 0:w - 1], in1=fac[:, g, 0:w - 1], op=ALU.mult)))(),
                           reads=[B(f"sS{g}"), B("fac")], writes=[B(f"sS{g}")])

        def stage_Dmm(i, pre=True, main=True):
            par = i % 2
            for g, w in enumerate(POOL_W):
                if not pre:
                    break
                S.emit("dve", (lambda g=g, w=w: (lambda e: e.scalar_tensor_tensor(out=sbf[:, g, :], in0=ub[par][:, g, UH:UH + T], scalar=-float(w),
                                                                                 in1=sS[:, g, :], op0=ALU.mult, op1=ALU.add)))(),
                       reads=[B(f"um{par}_{g}"), B(f"sS{g}")], writes=[B(f"sbf{g}")])
            if not main:
                return
            bks = [next_bank(), next_bank()]
            fns = [(lambda g=g, bk=bks[g // 2]: (lambda e: e.matmul(out=banks[bk][:, (g % 2) * T:(g % 2 + 1) * T], lhsT=pwa_bf[:, g, :], rhs=sbf[:, g, :],
                                                                 start=True, stop=True)))()
                   for g in range(4)]
            S.emit("pe", fns, reads=[B("wts")] + [B(f"sbf{g}") for g in range(4)], writes=[bankB[b_] for b_ in bks])
            for g in range(4):
                bk = bks[g // 2]
                S.emit("dve", (lambda g=g, bk=bk: (lambda e: e.scalar_tensor_tensor(out=yT[par][:, g, :], in0=banks[bk][:, (g % 2) * T:(g % 2 + 1) * T],
                                                                                  scalar=cols[:, C_PB + g:C_PB + g + 1], in1=ga[:, g, :],
                                                                                  op0=ALU.add, op1=ALU.mult)))(),
                       reads=[bankB[bk], B(f"ga{g}"), B("cols")], writes=[B(f"yT{par}_{g}")])

        def stage_stack(i):
            par = i % 2
            pst = hs[:, :, :].ap[0][0]
            rd = [B(f"hm{par}_{q}") for q in range(4)] + [B(f"hh{par}_{q}") for q in range(4)] + [B(f"hpad{par}")]
            hpst = hb[par][:, :, :].ap[0][0]
            HL = HH + T + 2
            RUN = 4 * HL - 4
            for ib in range(4):
                for j in range(4):
                    dst = bass.AP(hs, 32 * j * pst + ib * 4 * HL, [[pst, 32], [1, RUN]])
                    src = bass.AP(hb[par], 32 * ib * hpst + j, [[hpst, 32], [1, RUN]])
                    last_tok = S.emit("sp", (lambda dst=dst, src=src: (lambda e: e.dma_start(out=dst, in_=src)))(),
                                      reads=rd, writes=[B(f"hs_{ib}_{j}")], dma_sem=sem_hs)
            for ib in range(4):
                for j in range(4):
                    B(f"hs_{ib}_{j}").writer = last_tok

        def stage_Epe(i):
            fns = []
            for m in range(NGRP):
                for blk in range(16):
                    q, cj = blk // 4, blk % 4
                    fns.append((lambda m=m, blk=blk, q=q, cj=cj: (lambda e: e.matmul(
                        out=banks[q][32 * cj:32 * cj + 32, 0:T],
                        lhsT=Wd[:, blk * NGRP + m, :],
                        rhs=hs[:, cj * 4 + q, 4 * m:4 * m + T],
                        start=(m == 0), stop=(m == NGRP - 1), tile_position=(0, 32 * cj))))())
            rd = [B("Wd")] + [B(f"hs_{ib}_{j}") for ib in range(4) for j in range(4)]
            S.emit("pe", fns, reads=rd, writes=[bankB[b] for b in range(4)])

        def stage_Eevac(i):
            for b in range(4):
                S.emit("act", (lambda b=b: (lambda e: e.activation(out=cbf[:, b, :], in_=banks[b][:, 0:T], func=AF.Identity,
                                                                  bias=cols[:, C_CB + b:C_CB + b + 1])))(),
                       reads=[bankB[b], B("cols")], writes=[B(f"cbf{b}")])
            for b in range(4):
                S.emit("act", (lambda b=b: (lambda e: e.activation(out=csq[:, b, :], in_=banks[b][:, 0:T], func=AF.Square,
                                                                  bias=cols[:, C_CB + b:C_CB + b + 1])))(),
                       reads=[bankB[b], B("cols")], writes=[B(f"csq{b}")])

        def stage_Eevac32(i):
            for b in range(4):
                S.emit("act", (lambda b=b: (lambda e: e.activation(out=c32[:, b, :], in_=banks[b][:, 0:T], func=AF.Identity,
                                                                  bias=cols[:, C_CB + b:C_CB + b + 1])))(),
                       reads=[bankB[b], B("cols")], writes=[B(f"c32_{b}")])

        def stage_F1(i):
            bk_mu = next_bank()
            bk_sq = next_bank()
            fns = [(lambda iq=iq: (lambda e: e.matmul(out=banks[bk_mu][:, 0:T], lhsT=ones_bf[:, :], rhs=cbf[:, iq, :], start=(iq == 0), stop=(iq == 3))))()
                   for iq in range(4)]
            S.emit("pe", fns, reads=[B("ones_bf")] + [B(f"cbf{iq}") for iq in range(4)], writes=[bankB[bk_mu]])
            fns = [(lambda iq=iq: (lambda e: e.matmul(out=banks[bk_sq][:, 0:T], lhsT=ones_bf[:, :], rhs=csq[:, iq, :], start=(iq == 0), stop=(iq == 3))))()
                   for iq in range(4)]
            S.emit("pe", fns, reads=[B("ones_bf")] + [B(f"csq{iq}") for iq in range(4)], writes=[bankB[bk_sq]])
            S.emit("act", lambda e: e.activation(out=mu_sb[:, :], in_=banks[bk_mu][:, 0:T], func=AF.Copy),
                   reads=[bankB[bk_mu]], writes=[B("mu_sb")])
            S.emit("act", lambda e: e.activation(out=t1[:, :], in_=banks[bk_mu][:, 0:T], func=AF.Square),
                   reads=[bankB[bk_mu]], writes=[B("t1")])
            S.emit("dve", lambda e: e.scalar_tensor_tensor(out=vareps[:, :], in0=banks[bk_sq][:, 0:T], scalar=LN_EPS, in1=t1[:, :],
                                                           op0=ALU.add, op1=ALU.subtract),
                   reads=[bankB[bk_sq], B("t1")], writes=[B("vareps")])
            S.emit("act", lambda e: e.activation(out=nw_a[:, :], in_=vareps[:, :], func=AF.Sqrt),
                   reads=[B("vareps")], writes=[B("nw_a")])

        def stage_F1b(i):
            S.emit("dve", lambda e: e.reciprocal(out=rstd_ln[:, :], in_=nw_a[:, :]),
                   reads=[B("nw_a")], writes=[B("rstd_ln")])
            S.emit("dve", lambda e: e.scalar_tensor_tensor(out=nmr[:, :], in0=mu_sb[:, :], scalar=-1.0, in1=rstd_ln[:, :],
                                                           op0=ALU.mult, op1=ALU.mult),
                   reads=[B("mu_sb"), B("rstd_ln")], writes=[B("nmr")])

        def stage_F2dve(i, pair):
            cp = c32[:, 2 * pair:2 * pair + 2, :]
            pst = list(rstd_ln[:, :].ap[0])
            r_b = bass.AP(rstd_ln, 0, [pst, [0, 2], [1, T]])
            n_b = bass.AP(nmr, 0, [list(nmr[:, :].ap[0]), [0, 2], [1, T]])
            bufs = [B(f"c32_{2 * pair}"), B(f"c32_{2 * pair + 1}")]
            S.emit("dve", lambda e: e.tensor_tensor(out=cp, in0=cp, in1=r_b, op=ALU.mult), reads=bufs + [B("rstd_ln")], writes=bufs)
            S.emit("dve", lambda e: e.tensor_tensor(out=cp, in0=cp, in1=n_b, op=ALU.add), reads=bufs + [B("nmr")], writes=bufs)

        def stage_F2act(i, iqs):
            for iq in iqs:
                S.emit("act", (lambda iq=iq: (lambda e: e.activation(out=s_bf[:, iq, :], in_=c32[:, iq, :], func=AF.Silu,
                                                                    scale=cols[:, C_LG + iq:C_LG + iq + 1], bias=cols[:, C_LB + iq:C_LB + iq + 1])))(),
                       reads=[B(f"c32_{iq}"), B("cols")], writes=[B(f"s_bf{iq}")])

        nw_a = sb("nw_a", [128, T], F32)
        def stage_G(i, fixed_banks=None):
            par = i % 2
            for o in range(4):
                bk = next_bank() if fixed_banks is None else fixed_banks[o]
                fns = [(lambda iq=iq, o=o, bk=bk: (lambda e: e.matmul(out=banks[bk][:, 0:T], lhsT=pw_bf[:, iq, o * 128:(o + 1) * 128], rhs=s_bf[:, iq, :],
                                                                     start=(iq == 0), stop=(iq == 3))))() for iq in range(4)]
                S.emit("pe", fns, reads=[B("pww")] + [B(f"s_bf{iq}") for iq in range(4)], writes=[bankB[bk]])
                S.emit("dve", (lambda o=o, bk=bk: (lambda e: e.scalar_tensor_tensor(out=yT[par][:, 4 + o, :], in0=banks[bk][:, 0:T],
                                                                                  scalar=cols[:, C_PWB + o:C_PWB + o + 1], in1=gb[par][:, o, :],
                                                                                  op0=ALU.add, op1=ALU.mult)))(),
                       reads=[bankB[bk], B(f"gb{par}_{o}"), B("cols")], writes=[B(f"yT{par}_{4 + o}")])

        h1_banks = {}

        def stage_H1(i, phase="both", fixed_banks=None):
            par = i % 2
            slot = i % NXB
            for j in range(2):
                for hf in range(2):
                    if phase in ("both", "first"):
                        bk = next_bank() if fixed_banks is None else fixed_banks[2 * j + hf]
                        h1_banks[(i, j, hf)] = bk
                    else:
                        bk = h1_banks[(i, j, hf)]
                    fns = [(lambda e_=e_, j=j, hf=hf, bk=bk: (lambda e: e.matmul(out=banks[bk][:, :], lhsT=yT[par][:, e_, j * 128:(j + 1) * 128],
                                                                                rhs=w_out_bf[:, e_, hf * 512:(hf + 1) * 512],
                                                                                start=(e_ == 0), stop=(e_ == 7))))() for e_ in range(8)]
                    if phase in ("both", "first"):
                        S.emit("pe", fns[:4], reads=[B("wout")] + [B(f"yT{par}_{e_}") for e_ in range(4)], writes=[bankB[bk]])
                    if phase in ("both", "second"):
                        S.emit("pe", fns[4:], reads=[B("wout")] + [B(f"yT{par}_{e_}") for e_ in range(4, 8)], writes=[bankB[bk]])
                        S.emit("dve", (lambda j=j, hf=hf, bk=bk: (lambda e: e.tensor_tensor(out=xb[slot][:, j, hf * 512:(hf + 1) * 512],
                                                                                           in0=banks[bk][:, :], in1=xb[slot][:, j, hf * 512:(hf + 1) * 512],
                                                                                           op=ALU.add)))(),
                               reads=[bankB[bk], B(f"xb{slot}_{j}_{hf}")], writes=[B(f"xb{slot}_{j}_{hf}")])

        def stage_H2(i):
            slot = i % NXB
            for j in range(2):
                S.emit("act", (lambda j=j: (lambda e: e.activation(out=junk[:, :], in_=xb[slot][:, j, :], func=AF.Square,
                                                                  accum_out=ssq2[:, 2 * i + j:2 * i + j + 1])))(),
                       reads=[B(f"xb{slot}_{j}_0"), B(f"xb{slot}_{j}_1")], writes=[B("junk"), B(f"ssq2{2 * i}")])
            rsqrt_small(ssq2, ms2, rstd2, 2 * i, RMS_EPS, 1.0 / D, "ssq2", "ms2", "rstd2")
            for j in range(2):
                S.emit("dve", (lambda j=j: (lambda e: e.scalar_tensor_tensor(out=xb[slot][:, j, :], in0=xb[slot][:, j, :],
                                                                            scalar=rstd2[:, 2 * i + j:2 * i + j + 1], in1=gfin[:, :],
                                                                            op0=ALU.mult, op1=ALU.mult)))(),
                       reads=[B(f"xb{slot}_{j}_0"), B(f"xb{slot}_{j}_1"), B(f"rstd2{2 * i}"), B("gfin")],
                       writes=[B(f"xb{slot}_{j}_0"), B(f"xb{slot}_{j}_1")])
            return S.emit("sp", lambda e: e.dma_start(out=out_t[i], in_=xb[slot][:, :, :]), reads=xbufs(slot), dma_sem=sem_xs[slot])

        def stage_H2_split(i):
            slot = i % NXB
            toks = []
            for j in range(2):
                c = 2 * i + j
                S.emit("act", (lambda j=j, c=c: (lambda e: e.activation(out=junk[:, :], in_=xb[slot][:, j, :], func=AF.Square,
                                                                       accum_out=ssq2[:, c:c + 1])))(),
                       reads=[B(f"xb{slot}_{j}_0"), B(f"xb{slot}_{j}_1")], writes=[B("junk"), B(f"ssq2s{c}")])
            for j in range(2):
                c = 2 * i + j
                S.emit("pool", (lambda c=c: (lambda e: e.tensor_scalar(out=ms2[:, c:c + 1], in0=ssq2[:, c:c + 1], scalar1=1.0 / D, scalar2=RMS_EPS,
                                                                      op0=ALU.mult, op1=ALU.add)))(),
                       reads=[B(f"ssq2s{c}")], writes=[B(f"ms2s{c}")])
                S.emit("pool", (lambda c=c: (lambda e: e.tensor_tensor(out=rstd2[:, c:c + 1], in0=ms2[:, c:c + 1], in1=neghalf[:, 0:1], op=ALU.pow)))(),
                       reads=[B(f"ms2s{c}"), B("neghalf")], writes=[B(f"rstd2s{c}")])
                S.emit("dve", (lambda j=j, c=c: (lambda e: e.scalar_tensor_tensor(out=xb[slot][:, j, :], in0=xb[slot][:, j, :],
                                                                                 scalar=rstd2[:, c:c + 1], in1=gfin[:, :],
                                                                                 op0=ALU.mult, op1=ALU.mult)))(),
                       reads=[B(f"xb{slot}_{j}_0"), B(f"xb{slot}_{j}_1"), B(f"rstd2s{c}"), B("gfin")],
                       writes=[B(f"xb{slot}_{j}_0"), B(f"xb{slot}_{j}_1")])
                toks.append(S.emit("sp", (lambda j=j: (lambda e: e.dma_start(out=out_t[i][:, j, :], in_=xb[slot][:, j, :])))(),
                                   reads=[B(f"xb{slot}_{j}_0"), B(f"xb{slot}_{j}_1")], dma_sem=sem_xs[slot]))
            return toks[-1]

        store_toks = []
        ntiles = NT if debug is None else debug.get("_ntiles", NT)
        def dumps(i):
            par = i % 2
            dbg("xT", xT[par][:, :, :], [128, 8, T], BF16, [B(f"xT{par}_0"), B(f"xT{par}_1")])
            dbg("ub", ub[par][:, :, :], [128, 4, UH + T + 1], F32, [B(f"um{par}_{g}") for g in range(4)])
            dbg("hb", hb[par][:, :, :], [128, 4, HH + T + 2], BF16, [B(f"hm{par}_{g}") for g in range(4)])
            dbg("ga", ga[:, :, :], [128, 4, T], BF16, [B(f"ga{g}") for g in range(4)])
            dbg("gb", gb[par][:, :, :], [128, 4, T], BF16, [B(f"gb{par}_{g}") for g in range(4)])
            dbg("sbf", sbf[:, :, :], [128, 4, T], BF16, [B(f"sbf{g}") for g in range(4)])
            dbg("cbf", cbf[:, :, :], [128, 4, T], BF16, [B(f"cbf{g}") for g in range(4)])
            dbg("rstd_ln", rstd_ln[:, :], [128, T], F32, [B("rstd_ln")])
            dbg("mu_sb", mu_sb[:, :], [128, T], F32, [B("mu_sb")])
            dbg("s_bf", s_bf[:, :, :], [128, 4, T], BF16, [B(f"s_bf{g}") for g in range(4)])
            dbg("yT", yT[par][:, :, :], [128, 8, T], BF16, [B(f"yT{par}_{g}") for g in range(8)])
            dbg("rstdx", rstdx[:, :], [128, 2 * NT], F32, [B(f"rstdx{2 * i}")])

        def tile0_pre():
            stage_Aload(0)
            stage_Acomp(0)
        emit_win("A", hook3=tile0_pre)
        stage_B(0)
        emit_win("B")
        if ntiles > 1:
            stage_Aload(1)
        stage_C(0, ("glu",))
        emit_win("C_dma")
        wide_stage[0] = False
        stage_i[0] = 0
        if ntiles > 1:
            stage_Acomp_a(1)
        stage_C(0, ("bval",))
        stage_stack(0)
        if ntiles > 1:
            stage_Acomp_b(1)
        emit_win("C_cast")
        emit_poolpw_dma(3, 4)
        emit_wout_dma(0, 2)
        stage_C(0, ("aval",))
        stage_Dpool(0)
        stage_C(0, ("agate",))
        stage_Dmm(0, main=False)
        emit_poolpw_cast(3, 4)
        hs_done = [B(f"hs_{ib}_{j}") for ib in range(4) for j in range(4)]
        emit_wout_dma(1, 3, after=hs_done)
        emit_wout_dma(2, 4, after=hs_done)
        stage_C(0, ("bgate",))
        if ntiles > 1:
            stage_B(1)
        emit_wout_cast(0)
        if ntiles > 2:
            stage_Aload(2)
        stage_Epe(0)
        stage_Eevac(0)
        stage_Dmm(0, pre=False)
        if ntiles == 1:
            emit_gfin()
            emit_wout_cast(1)
            emit_wout_cast(2)
            emit_wout_dma(3, 3)
            emit_wout_cast(3)
        for i in range(ntiles):
            n1 = i + 1 < ntiles
            n2 = i + 2 < ntiles
            if n1:
                stage_C(i + 1, ("glu",))
            if i == 0 and ntiles > 1:
                emit_wout_cast(1, eng="dve")
                emit_wout_dma(3, 3)
                emit_gfin()
            stage_F1(i)
            stage_Eevac32(i)
            if ntiles >= 3 and i == ntiles - 1:
                stage_G(i - 1)
            if n2:
                stage_Acomp_a(i + 2)
            if n1:
                stage_C(i + 1, ("bval",))
                stage_stack(i + 1)
            stage_F1b(i)
            if i == 0 and ntiles > 1:
                emit_wout_cast(2, eng="dve")
                emit_wout_cast(3, eng="dve")
            if n2:
                stage_Acomp_b(i + 2)
            if n1:
                stage_C(i + 1, ("aval",))
                stage_Dpool(i + 1)
            if n1:
                stage_C(i + 1, ("agate",))
            stage_F2dve(i, 0)
            stage_F2dve(i, 1)
            stage_F2act(i, (0, 1))
            if n1:
                stage_C(i + 1, ("bgate",), qs=(0, 1))
            stage_F2act(i, (2,))
            if n1:
                stage_C(i + 1, ("bgate",), qs=(2, 3))
            stage_F2act(i, (3,))
            if n2:
                stage_B(i + 2)
            last_fill = (ntiles >= 3 and i == ntiles - 1)
            if last_fill:
                stage_H1(i - 1)
                stage_H1(i, phase="first", fixed_banks=(0, 1, 2, 3))
                stage_G(i)
            elif not (ntiles >= 3 and i == ntiles - 2):
                stage_G(i)
            if i > 0:
                store_toks.append(stage_H2(i - 1))
            if i + 3 < ntiles:
                stage_Aload(i + 3)
            defer_h1 = (ntiles >= 3 and i == ntiles - 2)
            if last_fill:
                stage_H1(i, phase="second")
            elif not defer_h1:
                stage_H1(i)
            if n1:
                stage_Epe(i + 1)
                stage_Eevac(i + 1)
                stage_Dmm(i + 1)
            if debug is not None and i == debug.get("_tile", 0):
                dumps(i)
        store_toks.append(stage_H2_split(ntiles - 1))
        final = {}
        for sem, val in store_toks:
            final[sem] = max(final.get(sem, 0), val)
        if sem_dbg.count:
            final[sem_dbg] = sem_dbg.count
        S.wait_only("sp", list(final.items()))

        with nc.Block() as block:
            @block.sync
            def _(sync):
                S.replay("sp", sync)

            @block.tensor
            def _(tensor):
                S.replay("pe", tensor)

            @block.scalar
            def _(scalar):
                S.replay("act", scalar)

            @block.vector
            def _(vector):
                S.replay("dve", vector)

            @block.gpsimd
            def _(gpsimd):
                S.replay("pool", gpsimd)

    return nc, dbg_out


def _perm_cols(v):
    return np.ascontiguousarray(v.reshape(4, 128).T)


def prepare_inputs(x, norm_g, w_in, pool_w, pool_b, pool_scale, conv_dw, conv_b,
                   ln_g, ln_b, pw_w, pw_b, w_out, final_g):
    f = lambda a: np.ascontiguousarray(np.asarray(a, dtype=np.float32))
    x = f(x)
    cols = np.zeros((128, NCOLS), np.float32)
    cols[:, C_NG:C_NG + 8] = f(norm_g).reshape(8, 128).T
    cols[:, C_PS:C_PS + 4] = f(pool_scale).reshape(4, 128).T
    cols[:, C_PB:C_PB + 4] = f(pool_b).reshape(4, 128).T
    cols[:, C_CB:C_CB + 4] = _perm_cols(f(conv_b))
    cols[:, C_LG:C_LG + 4] = _perm_cols(f(ln_g))
    cols[:, C_LB:C_LB + 4] = _perm_cols(f(ln_b))
    cols[:, C_PWB:C_PWB + 4] = f(pw_b).reshape(4, 128).T
    cw4 = np.zeros((NGRP * 4, 512), np.float32)
    cw4[:KW] = f(conv_dw)
    cols[:, C_CW:] = cw4.reshape(NGRP, 4, 16, 32).transpose(1, 3, 2, 0).reshape(128, 16 * NGRP)
    pww = f(pw_w).reshape(4, 128, 512).transpose(1, 0, 2).reshape(128, 4 * 512)
    plw = f(pool_w).transpose(1, 0, 2).reshape(128, 4 * 128)
    gfin = np.broadcast_to(f(final_g)[None, :], (128, D))
    shared = {
        "w_in": f(w_in), "w_out": f(w_out), "pw_w": np.ascontiguousarray(pww),
        "pool_w": np.ascontiguousarray(plw), "cols": cols, "gfin": np.ascontiguousarray(gfin),
    }
    xs = x.reshape(NCORES, TOK_PER_CORE, D)
    in_maps = []
    for c in range(NCORES):
        m = dict(shared)
        m["x"] = np.ascontiguousarray(xs[c])
        in_maps.append(m)
    return in_maps


def kernel(x, norm_g, w_in, pool_w, pool_b, pool_scale, conv_dw, conv_b,
           ln_g, ln_b, pw_w, pw_b, w_out, final_g):
    in_maps = prepare_inputs(x, norm_g, w_in, pool_w, pool_b, pool_scale, conv_dw, conv_b,
                             ln_g, ln_b, pw_w, pw_b, w_out, final_g)
    nc, _ = build_nc()
    res = run_bass_kernel_spmd(nc, in_maps, core_ids=list(range(NCORES)))
    out = np.stack([np.asarray(r["out"], dtype=np.float32) for r in res.results], axis=0)
    return out.reshape(16, SEQ, D)
```

```python
import contextlib
import numpy as np
import concourse.bass as bass
import concourse.mybir as mybir
from concourse.bass_utils import run_bass_kernel_spmd

F32 = mybir.dt.float32
BF16 = mybir.dt.bfloat16
I32 = mybir.dt.int32
AF = mybir.ActivationFunctionType
ALU = mybir.AluOpType

NCORES = 8
D = 1024
DIN = 2560
SEQ = 2048
TOK_PER_CORE = 2 * SEQ
T = 256
NT = TOK_PER_CORE // T
TPS = SEQ // T
KW = 31
POOL_W = (2, 4, 8, 16)
UH = 15
HH = 30
RMS_EPS = 1e-6
LN_EPS = 1e-5

C_NG, C_PS, C_PB, C_CB, C_LG, C_LB, C_PWB, C_CW = 0, 8, 12, 16, 20, 24, 28, 32
NGRP = 8
NCOLS = C_CW + 16 * NGRP
LS = T + 28


class Sem:
    def __init__(self, h, name):
        self.h = h
        self.name = name
        self.count = 0


class Buf:
    __slots__ = ("name", "writer", "readers")

    def __init__(self, name):
        self.name = name
        self.writer = None
        self.readers = {}


class Eng:
    def __init__(self, name, sem):
        self.name = name
        self.sem = sem
        self.ops = []
        self.waited = {}


class Sched:
    def __init__(self, nc, stack):
        self.nc = nc
        self.stack = stack
        self.engs = {}
        for n in ("pe", "act", "dve", "pool", "sp"):
            self.engs[n] = Eng(n, self.new_sem("e_" + n))
        self.bufs = {}

    def new_sem(self, name):
        return Sem(self.stack.enter_context(self.nc.semaphore(name)), name)

    def buf(self, name):
        b = self.bufs.get(name)
        if b is None:
            b = Buf(name)
            self.bufs[name] = b
        return b

    def emit(self, eng, fns, reads=(), writes=(), dma_sem=None):
        e = self.engs[eng]
        if not isinstance(fns, (list, tuple)):
            fns = [fns]
        waits = {}

        def need(tok):
            if tok is None:
                return
            sem, val = tok
            if eng == "pe" and sem is e.sem:
                return
            if e.waited.get(sem, 0) >= val:
                return
            if waits.get(sem, 0) < val:
                waits[sem] = val

        for b in reads:
            need(b.writer)
        for b in writes:
            need(b.writer)
            for s, v in b.readers.items():
                need((s, v))
        for s, v in waits.items():
            e.waited[s] = v
        if dma_sem is None:
            e.sem.count += 1
            tok = (e.sem, e.sem.count)
            amt = 1
        else:
            dma_sem.count += 16
            tok = (dma_sem, dma_sem.count)
            amt = 16
        e.ops.append((list(waits.items()), list(fns), tok[0], amt))
        for b in reads:
            if b.readers.get(tok[0], 0) < tok[1]:
                b.readers[tok[0]] = tok[1]
        for b in writes:
            b.writer = tok
            b.readers = {}
        return tok

    def wait_only(self, eng, toks):
        e = self.engs[eng]
        waits = {}
        for sem, val in toks:
            if e.waited.get(sem, 0) >= val:
                continue
            if waits.get(sem, 0) < val:
                waits[sem] = val
        for s, v in waits.items():
            e.waited[s] = v
        e.ops.append((list(waits.items()), [], None, 0))

    def replay(self, eng, handle):
        for waits, fns, sem, amt in self.engs[eng].ops:
            for s, v in waits:
                handle.wait_ge(s.h, v)
            ins = None
            for f in fns:
                ins = f(handle)
            if ins is not None and sem is not None:
                ins.then_inc(sem.h, amt)


def build_nc(debug=None):
    nc = bass.Bass("TRN2", target_bir_lowering=False)
    dram = {}

    def din(name, shape):
        dram[name] = nc.dram_tensor(name, shape, F32, kind="ExternalInput")
        return dram[name]

    x_d = din("x", [TOK_PER_CORE, D])
    win_d = din("w_in", [D, DIN])
    wout_d = din("w_out", [D, D])
    pww_d = din("pw_w", [128, 4 * 512])
    plw_d = din("pool_w", [128, 4 * 128])
    cols_d = din("cols", [128, NCOLS])
    gfin_d = din("gfin", [128, D])
    out_d = nc.dram_tensor("out", [TOK_PER_CORE, D], F32, kind="ExternalOutput")
    x_t = x_d.ap().rearrange("(n j p) d -> n p j d", p=128, j=2)
    out_t = out_d.ap().rearrange("(n j p) d -> n p j d", p=128, j=2)

    dbg_out = {}

    with contextlib.ExitStack() as st:
        S = Sched(nc, st)

        def sb(name, shape, dt):
            return st.enter_context(nc.sbuf_tensor("sb_" + name, shape, dt))

        cols = sb("cols", [128, NCOLS], F32)
        gfin = sb("gfin", [128, D], F32)
        ident_f = sb("ident_f", [128, 128], F32)
        ident_bf = sb("ident_bf", [128, 128], BF16)
        ones_bf = sb("ones_bf", [128, 128], BF16)
        identblk = sb("identblk", [128, 32], F32)
        cwh = sb("cwh", [128, 16 * NGRP], F32)
        Wd = sb("Wd", [128, 16 * NGRP, 32], BF16)
        hs = sb("hs", [128, 16, HH + T + 2], BF16)
        fac = sb("fac", [128, 4, 16], F32)
        neghalf = sb("neghalf", [128, 8], F32)
        w_in_bf = sb("w_in_bf", [128, 8, DIN], BF16)
        w_out_bf = sb("w_out_bf", [128, 8, D], BF16)
        pw_bf = sb("pw_bf", [128, 4, 512], BF16)
        pwa_bf = sb("pwa_bf", [128, 4, 128], BF16)
        NXB = 5
        xb = [sb(f"xb{i}", [128, 2, D], F32) for i in range(NXB)]
        xn = [sb(f"xn{i}", [128, 2, D], BF16) for i in range(2)]
        xT = [sb(f"xT{i}", [128, 8, T], BF16) for i in range(2)]
        ub = [sb(f"ub{i}", [128, 4, UH + T + 1], F32) for i in range(2)]
        sA = sb("sA", [128, UH + T + 1], F32)
        sB = sb("sB", [128, UH + T + 1], F32)
        sbf = sb("sbf", [128, 4, T], BF16)
        sS = sb("sS", [128, 4, T], F32)
        ga = sb("ga", [128, 4, T], BF16)
        gb = [sb(f"gb{i}", [128, 4, T], BF16) for i in range(2)]
        th = sb("th", [128, 4, T], F32)
        hb = [sb(f"hb{i}", [128, 4, HH + T + 2], BF16) for i in range(2)]
        c32 = sb("c32", [128, 4, T], F32)
        cbf = sb("cbf", [128, 4, T], BF16)
        csq = sb("csq", [128, 4, T], BF16)
        mu_sb = sb("mu_sb", [128, T], F32)
        t1 = sb("t1", [128, T], F32)
        vareps = sb("vareps", [128, T], F32)
        rstd_ln = sb("rstd_ln", [128, T], F32)
        nmr = sb("nmr", [128, T], F32)
        s_bf = sb("s_bf", [128, 4, T], BF16)
        yT = [sb(f"yT{i}", [128, 8, T], BF16) for i in range(2)]
        ssq = sb("ssq", [128, 2 * NT], F32)
        msx = sb("msx", [128, 2 * NT], F32)
        rstdx = sb("rstdx", [128, 2 * NT], F32)
        ssq2 = sb("ssq2", [128, 2 * NT], F32)
        ms2 = sb("ms2", [128, 2 * NT], F32)
        rstd2 = sb("rstd2", [128, 2 * NT], F32)
        junk = sb("junk", [128, D], BF16)

        banks = [st.enter_context(nc.psum_tensor(f"bank{i}", [128, 512], F32)) for i in range(8)]
        bankB = [S.buf(f"bank{i}") for i in range(8)]
        gen_state = {"next": 0}
        GEN_BANKS = (4, 5, 6, 7)

        def next_bank():
            b = GEN_BANKS[gen_state["next"] % len(GEN_BANKS)]
            gen_state["next"] += 1
            return b

        B = S.buf
        sem_small = S.new_sem("d_small")
        sem_small2 = S.new_sem("d_small2")
        sem_hs = S.new_sem("d_hs")
        sem_xl = [S.new_sem(f"d_xl{i}") for i in range(NXB)]
        sem_xs = [S.new_sem(f"d_xs{i}") for i in range(NXB)]
        sem_dbg = S.new_sem("d_dbg")

        def xbufs(slot):
            return [B(f"xb{slot}_{j}_{h}") for j in range(2) for h in range(2)]

        def dbg(name, tensor, shape, dt, reads):
            if debug is None or name not in debug:
                return
            dd = nc.dram_tensor("dbg_" + name, shape, dt, kind="ExternalOutput")
            dbg_out[name] = dd
            S.emit("sp", lambda e: e.dma_start(out=dd.ap(), in_=tensor), reads=reads, dma_sem=sem_dbg)

        S.emit("sp", lambda e: e.dma_start(out=cols[:, :], in_=cols_d.ap()), writes=[B("cols")], dma_sem=sem_small)
        def emit_gfin():
            S.emit("sp", lambda e: e.dma_start(out=gfin[:, :], in_=gfin_d.ap()), writes=[B("gfin")], dma_sem=sem_small2)

        S.emit("pool", lambda e: e.memset(ident_f[:, :], 0.0), writes=[B("ident_f")])
        S.emit("pool", lambda e: e.affine_select(out=ident_f[:, :], in_=ident_f[:, :], pattern=[[-1, 128]],
                                                  compare_op=ALU.not_equal, fill=1.0, base=0, channel_multiplier=1),
               reads=[B("ident_f")], writes=[B("ident_f")])
        S.emit("pool", lambda e: e.tensor_copy(out=ident_bf[:, :], in_=ident_f[:, :]), reads=[B("ident_f")], writes=[B("ident_bf")])
        S.emit("pool", lambda e: e.memset(ones_bf[:, :], 1.0 / 512.0), writes=[B("ones_bf")])
        S.emit("pool", lambda e: e.memset(neghalf[:, :], -0.5), writes=[B("neghalf")])
        S.emit("pool", lambda e: e.tensor_tensor(out=identblk[:, :], in0=ident_f[:, 0:32], in1=ident_f[:, 32:64], op=ALU.add),
               reads=[B("ident_f")], writes=[B("identblk")])
        S.emit("pool", lambda e: e.tensor_tensor(out=identblk[:, :], in0=identblk[:, :], in1=ident_f[:, 64:96], op=ALU.add),
               reads=[B("ident_f"), B("identblk")], writes=[B("identblk")])
        S.emit("pool", lambda e: e.tensor_tensor(out=identblk[:, :], in0=identblk[:, :], in1=ident_f[:, 96:128], op=ALU.add),
               reads=[B("ident_f"), B("identblk")], writes=[B("identblk")])
        for par_ in range(2):
            S.emit("pool", (lambda par_=par_: (lambda e: e.memset(hb[par_][:, :, HH + T:HH + T + 2], 0.0)))(), writes=[B(f"hpad{par_}")])
        S.emit("pool", lambda e: e.memset(fac[:, :, :], 1.0), writes=[B("fac")])
        for g, w in enumerate(POOL_W):
            for t in range(w - 1):
                S.emit("pool", (lambda g=g, t=t, w=w: (lambda e: e.memset(fac[:, g, t:t + 1], float(w) / float(t + 1))))(),
                       reads=[B("fac")], writes=[B("fac")])
        S.emit("pool", lambda e: e.tensor_scalar(out=cwh[:, :], in0=cols[:, C_CW:C_CW + 16 * NGRP], scalar1=0.5, scalar2=0.0, op0=ALU.mult, op1=ALU.add),
               reads=[B("cols")], writes=[B("cwh")])
        ib_ap = identblk[:, :]
        cw_ap = cwh[:, :]
        ib_b = bass.AP(identblk, 0, [list(ib_ap.ap[0]), [0, 16 * NGRP], [1, 32]])
        cw_b = bass.AP(cwh, 0, [list(cw_ap.ap[0]), [1, 16 * NGRP], [0, 32]])
        S.emit("pool", lambda e: e.tensor_tensor(out=Wd[:, :, :], in0=ib_b, in1=cw_b, op=ALU.mult),
               reads=[B("identblk"), B("cwh")], writes=[B("Wd")])

        stage_slots = [3, 4]
        stage_sem = {2: S.new_sem("d_st2"), 3: S.new_sem("d_st3"), 4: S.new_sem("d_st4")}
        stage_i = [0]
        cast_i = [0]

        wide_stage = [True]

        def next_stage():
            lst = (2, 3, 4) if wide_stage[0] else (3, 4)
            slot = lst[stage_i[0] % len(lst)]
            stage_i[0] += 1
            return slot

        def cast_op(dst_ap, src_ap, scale, rd, wr, eng=None):
            if eng is None:
                eng = ("dve", "act", "dve")[cast_i[0] % 3]
                cast_i[0] += 1
            if eng == "act":
                if scale is None:
                    fn = lambda e: e.activation(out=dst_ap, in_=src_ap, func=AF.Copy)
                else:
                    fn = lambda e: e.activation(out=dst_ap, in_=src_ap, func=AF.Copy, scale=scale)
            else:
                if scale is None:
                    fn = lambda e: e.tensor_copy(out=dst_ap, in_=src_ap)
                else:
                    fn = lambda e: e.tensor_scalar(out=dst_ap, in0=src_ap, scalar1=scale, scalar2=None, op0=ALU.mult)
            S.emit(eng, fn, reads=rd, writes=wr)

        winC_slot = {}

        def emit_win(blk, hook3=None):
            pst = list(xb[0][:, :, :].ap[0])
            if blk in ("A", "B"):
                c0 = 1024 if blk == "A" else 0
                blks = (2, 3) if blk == "A" else (0, 1)
                for pr in range(4):
                    if pr == 3 and hook3 is not None:
                        hook3()
                    slot = next_stage()
                    sv = xb[slot][:, :, :]
                    src = win_d.ap()[pr * 256:(pr + 1) * 256, c0:c0 + 1024].rearrange("(c p) n -> p c n", p=128)
                    S.emit("sp", (lambda sv=sv, src=src: (lambda e: e.dma_start(out=sv, in_=src)))(), writes=xbufs(slot), dma_sem=stage_sem[slot])
                    for cl in range(2):
                        c = 2 * pr + cl
                        cast_op(w_in_bf[:, c, c0:c0 + 1024], xb[slot][:, cl, :], cols[:, C_NG + c:C_NG + c + 1],
                                xbufs(slot) + [B("cols")], [B(f"win{blks[0]}_{pr // 2}"), B(f"win{blks[1]}_{pr // 2}")])
            elif blk == "C_dma":
                for half in range(2):
                    slot = next_stage()
                    winC_slot[half] = slot
                    sv = bass.AP(xb[slot], 0, [pst, [512, 4], [1, 512]])
                    src = win_d.ap()[half * 512:(half + 1) * 512, 2048:2560].rearrange("(c p) n -> p c n", p=128)
                    S.emit("sp", (lambda sv=sv, src=src: (lambda e: e.dma_start(out=sv, in_=src)))(), writes=xbufs(slot), dma_sem=stage_sem[slot])
            else:
                for half in range(2):
                    slot = winC_slot[half]
                    sv = bass.AP(xb[slot], 0, [pst, [512, 4], [1, 512]])
                    for cl in range(4):
                        c = half * 4 + cl
                        cast_op(w_in_bf[:, c, 2048:2560], sv[:, cl, :], cols[:, C_NG + c:C_NG + c + 1],
                                xbufs(slot) + [B("cols")], [B(f"win4_{half}")])

        def emit_poolpw_dma(slot_pl, slot_pw):
            pst = list(xb[0][:, :, :].ap[0])
            sv_pl = bass.AP(xb[slot_pl], 0, [pst, [1, 512]])
            S.emit("sp", lambda e: e.dma_start(out=sv_pl, in_=plw_d.ap()), writes=xbufs(slot_pl), dma_sem=stage_sem[slot_pl])
            sv = bass.AP(xb[slot_pw], 0, [pst, [1, 2048]])
            S.emit("sp", (lambda sv=sv: (lambda e: e.dma_start(out=sv, in_=pww_d.ap())))(), writes=xbufs(slot_pw), dma_sem=stage_sem[slot_pw])

        def emit_poolpw_cast(slot_pl, slot_pw):
            pst = list(xb[0][:, :, :].ap[0])
            sv_pl = bass.AP(xb[slot_pl], 0, [pst, [1, 512]])
            for g, w in enumerate(POOL_W):
                S.emit("dve", (lambda g=g, w=w: (lambda e: e.tensor_scalar(out=pwa_bf[:, g, :], in0=sv_pl[:, g * 128:(g + 1) * 128],
                                                                          scalar1=1.0 / w, scalar2=None, op0=ALU.mult)))(),
                       reads=xbufs(slot_pl), writes=[B("wts")])
            sv = bass.AP(xb[slot_pw], 0, [pst, [1, 2048]])
            pw_flat = bass.AP(pw_bf, 0, [list(pw_bf[:, :, :].ap[0]), [1, 2048]])
            for hf in range(2):
                cast_op(pw_flat[:, hf * 1024:(hf + 1) * 1024], sv[:, hf * 1024:(hf + 1) * 1024], None, xbufs(slot_pw), [B("pww")],
                        eng="dve")

        wout_slot = {}

        def emit_wout_dma(m, slot, after=()):
            wout_slot[m] = slot
            sv = xb[slot][:, :, :]
            src = wout_d.ap()[m * 256:(m + 1) * 256, :].rearrange("(c p) n -> p c n", p=128)
            S.emit("sp", (lambda sv=sv, src=src: (lambda e: e.dma_start(out=sv, in_=src)))(), reads=list(after), writes=xbufs(slot), dma_sem=stage_sem[slot])

        def emit_wout_cast(m, eng=None):
            slot = wout_slot[m]
            for cl in range(2):
                c = 2 * m + cl
                sc = cols[:, C_PS + c:C_PS + c + 1] if c < 4 else None
                cast_op(w_out_bf[:, c, :], xb[slot][:, cl, :], sc, xbufs(slot) + [B("cols")], [B("wout")], eng=eng)

        def rsqrt_small(src_ssq, ms_t, out_t, c0, eps, scale, nsrc, nms, nout):
            S.emit("pool", lambda e: e.tensor_scalar(out=ms_t[:, c0:c0 + 2], in0=src_ssq[:, c0:c0 + 2], scalar1=scale, scalar2=eps,
                                                      op0=ALU.mult, op1=ALU.add),
                   reads=[B(f"{nsrc}{c0}")], writes=[B(f"{nms}{c0}")])
            S.emit("pool", lambda e: e.tensor_tensor(out=out_t[:, c0:c0 + 2], in0=ms_t[:, c0:c0 + 2], in1=neghalf[:, 0:2], op=ALU.pow),
                   reads=[B(f"{nms}{c0}"), B("neghalf")], writes=[B(f"{nout}{c0}")])

        def stage_Aload(i):
            slot = i % NXB
            S.emit("sp", lambda e: e.dma_start(out=xb[slot][:, :, :], in_=x_t[i]), writes=xbufs(slot), dma_sem=sem_xl[slot])

        def stage_Acomp_a(i):
            slot = i % NXB
            par = i % 2
            for j in range(2):
                S.emit("act", (lambda j=j: (lambda e: e.activation(out=xn[par][:, j, :], in_=xb[slot][:, j, :], func=AF.Square,
                                                                  accum_out=ssq[:, 2 * i + j:2 * i + j + 1])))(),
                       reads=[B(f"xb{slot}_{j}_0"), B(f"xb{slot}_{j}_1")], writes=[B(f"xn{par}_{j}"), B(f"ssq{2 * i}")])
            rsqrt_small(ssq, msx, rstdx, 2 * i, RMS_EPS, 1.0 / D, "ssq", "msx", "rstdx")

        def stage_Acomp_b(i):
            slot = i % NXB
            par = i % 2
            for j in range(2):
                S.emit("dve", (lambda j=j: (lambda e: e.tensor_scalar(out=xn[par][:, j, :], in0=xb[slot][:, j, :],
                                                                     scalar1=rstdx[:, 2 * i + j:2 * i + j + 1], scalar2=None, op0=ALU.mult)))(),
                       reads=[B(f"xb{slot}_{j}_0"), B(f"xb{slot}_{j}_1"), B(f"rstdx{2 * i}")], writes=[B(f"xn{par}_{j}")])

        def stage_Acomp(i):
            stage_Acomp_a(i)
            stage_Acomp_b(i)

        def stage_B(i):
            par = i % 2
            for grp in range(2):
                bk = 2 + grp
                pbf = banks[bk].bitcast(BF16)
                fns = []
                for dcl in range(4):
                    dc = grp * 4 + dcl
                    for j in range(2):
                        fns.append((lambda dc=dc, dcl=dcl, j=j, pbf=pbf: (lambda e: e.transpose(
                            out=pbf[:, dcl * T + j * 128: dcl * T + (j + 1) * 128],
                            in_=xn[par][:, j, dc * 128:(dc + 1) * 128], identity=ident_bf[:, :])))())
                S.emit("pe", fns, reads=[B(f"xn{par}_0"), B(f"xn{par}_1"), B("ident_bf")], writes=[bankB[bk]])
                dst = bass.AP(xT[par], grp * 4 * T, [list(xT[par][:, :, :].ap[0]), [1, 4 * T]])
                S.emit("act", (lambda pbf=pbf, dst=dst: (lambda e: e.activation(out=dst, in_=pbf[:, 0:4 * T], func=AF.Copy)))(),
                       reads=[bankB[bk]], writes=[B(f"xT{par}_{grp}")])

        def proj_group(i, f):
            par = i % 2
            bk = next_bank()
            fns = []
            for dc in range(8):
                fns.append((lambda dc=dc, bk=bk: (lambda e: e.matmul(out=banks[bk][:, 0:T], lhsT=w_in_bf[:, dc, f * 128:(f + 1) * 128],
                                                                   rhs=xT[par][:, dc, :], start=(dc == 0), stop=(dc == 7))))())
            S.emit("pe", fns, reads=[B(f"win{f // 4}_0"), B(f"win{f // 4}_1"), B(f"xT{par}_0"), B(f"xT{par}_1")], writes=[bankB[bk]])
            return bk

        def stage_C(i, which, qs=(0, 1, 2, 3)):
            par = i % 2
            first = (i % TPS == 0)
            for kind in which:
                for q in qs:
                    if kind == "glu":
                        bk = proj_group(i, 12 + q)
                        S.emit("act", (lambda q=q, bk=bk: (lambda e: e.activation(out=th[:, q, :], in_=banks[bk][:, 0:T], func=AF.Tanh, scale=0.5)))(),
                               reads=[bankB[bk]], writes=[B(f"th{q}")])
                    elif kind == "bval":
                        if q == 0:
                            for qq in range(4):
                                if first:
                                    S.emit("pool", (lambda qq=qq: (lambda e: e.memset(hb[par][:, qq, 0:HH], 0.0)))(),
                                           writes=[B(f"hh{par}_{qq}")])
                                else:
                                    S.emit("pool", (lambda qq=qq: (lambda e: e.tensor_copy(out=hb[par][:, qq, 0:HH], in_=hb[1 - par][:, qq, T:T + HH])))(),
                                           reads=[B(f"hm{1 - par}_{qq}")], writes=[B(f"hh{par}_{qq}")])
                        bk = proj_group(i, 8 + q)
                        S.emit("dve", (lambda q=q, bk=bk: (lambda e: e.scalar_tensor_tensor(out=hb[par][:, q, HH:HH + T], in0=th[:, q, :], scalar=1.0,
                                                                                          in1=banks[bk][:, 0:T], op0=ALU.add, op1=ALU.mult)))(),
                               reads=[bankB[bk], B(f"th{q}")], writes=[B(f"hm{par}_{q}")])
                    elif kind == "aval":
                        if q == 0:
                            for gg in range(4):
                                if first:
                                    S.emit("pool", (lambda gg=gg: (lambda e: e.memset(ub[par][:, gg, 0:UH], 0.0)))(),
                                           writes=[B(f"uh{par}_{gg}")])
                                else:
                                    S.emit("pool", (lambda gg=gg: (lambda e: e.tensor_copy(out=ub[par][:, gg, 0:UH], in_=ub[1 - par][:, gg, T:T + UH])))(),
                                           reads=[B(f"um{1 - par}_{gg}")], writes=[B(f"uh{par}_{gg}")])
                        bk = proj_group(i, 0 + q)
                        S.emit("act", (lambda q=q, bk=bk: (lambda e: e.activation(out=ub[par][:, q, UH:UH + T], in_=banks[bk][:, 0:T], func=AF.Copy)))(),
                               reads=[bankB[bk]], writes=[B(f"um{par}_{q}")])
                    elif kind == "agate":
                        bk = proj_group(i, 4 + q)
                        S.emit("act", (lambda q=q, bk=bk: (lambda e: e.activation(out=ga[:, q, :], in_=banks[bk][:, 0:T], func=AF.Silu)))(),
                               reads=[bankB[bk]], writes=[B(f"ga{q}")])
                    elif kind == "bgate":
                        bk = proj_group(i, 16 + q)
                        S.emit("act", (lambda q=q, bk=bk: (lambda e: e.activation(out=gb[par][:, q, :], in_=banks[bk][:, 0:T], func=AF.Silu)))(),
                               reads=[bankB[bk]], writes=[B(f"gb{par}_{q}")])

        def stage_Dpool(i):
            par = i % 2
            first = (i % TPS == 0)
            for g, w in enumerate(POOL_W):
                nlev = g + 1
                src = ub[par]
                rd = [B(f"um{par}_{g}"), B(f"uh{par}_{g}")]
                cur_lo = 0
                cur = None
                for l in range(nlev):
                    sh = 1 << l
                    lo = cur_lo + sh
                    last = (l == nlev - 1)
                    if last:
                        lo = UH
                    n = UH + T - lo
                    if cur is None:
                        in_a = src[:, g, lo:lo + n]
                        in_b = src[:, g, lo - sh:lo - sh + n]
                        rds = rd
                    else:
                        in_a = cur[:, lo:lo + n]
                        in_b = cur[:, lo - sh:lo - sh + n]
                        rds = [B("sA" if cur is sA else "sB")]
                    if last:
                        out_ap = sS[:, g, :]
                        wr = [B(f"sS{g}")]
                        nxt = None
                    else:
                        nxt = sA if (cur is not sA) else sB
                        out_ap = nxt[:, lo:lo + n]
                        wr = [B("sA" if nxt is sA else "sB")]
                    S.emit("pool", (lambda out_ap=out_ap, in_a=in_a, in_b=in_b: (lambda e: e.tensor_tensor(out=out_ap, in0=in_a, in1=in_b, op=ALU.add)))(),
                           reads=rds, writes=wr)
                    cur = nxt
                    cur_lo = lo
                if first:
                    S.emit("pool", (lambda g=g, w=w: (lambda e: e.tensor_tensor(out=sS[:, g, 0:w - 1], in0=sS[:, g, 0:w - 1], in1=fac[:, g, 0:w - 1], op=ALU.mult)))(),
                           reads=[B(f"sS{g}"), B("fac")], writes=[B(f"sS{g}")])

        def stage_Dmm(i, pre=True, main=True):
            par = i % 2
            for g, w in enumerate(POOL_W):
                if not pre:
                    break
                S.emit("dve", (lambda g=g, w=w: (lambda e: e.scalar_tensor_tensor(out=sbf[:, g, :], in0=ub[par][:, g, UH:UH + T], scalar=-float(w),
                                                                                 in1=sS[:, g, :], op0=ALU.mult, op1=ALU.add)))(),
                       reads=[B(f"um{par}_{g}"), B(f"sS{g}")], writes=[B(f"sbf{g}")])
            if not main:
                return
            bks = [next_bank(), next_bank()]
            fns = [(lambda g=g, bk=bks[g // 2]: (lambda e: e.matmul(out=banks[bk][:, (g % 2) * T:(g % 2 + 1) * T], lhsT=pwa_bf[:, g, :], rhs=sbf[:, g, :],
                                                                 start=True, stop=True)))()
                   for g in range(4)]
            S.emit("pe", fns, reads=[B("wts")] + [B(f"sbf{g}") for g in range(4)], writes=[bankB[b_] for b_ in bks])
            for g in range(4):
                bk = bks[g // 2]
                S.emit("dve", (lambda g=g, bk=bk: (lambda e: e.scalar_tensor_tensor(out=yT[par][:, g, :], in0=banks[bk][:, (g % 2) * T:(g % 2 + 1) * T],
                                                                                  scalar=cols[:, C_PB + g:C_PB + g + 1], in1=ga[:, g, :],
                                                                                  op0=ALU.add, op1=ALU.mult)))(),
                       reads=[bankB[bk], B(f"ga{g}"), B("cols")], writes=[B(f"yT{par}_{g}")])

        def stage_stack(i):
            par = i % 2
            pst = hs[:, :, :].ap[0][0]
            rd = [B(f"hm{par}_{q}") for q in range(4)] + [B(f"hh{par}_{q}") for q in range(4)] + [B(f"hpad{par}")]
            hpst = hb[par][:, :, :].ap[0][0]
            HL = HH + T + 2
            RUN = 4 * HL - 4
            for ib in range(4):
                for j in range(4):
                    dst = bass.AP(hs, 32 * j * pst + ib * 4 * HL, [[pst, 32], [1, RUN]])
                    src = bass.AP(hb[par], 32 * ib * hpst + j, [[hpst, 32], [1, RUN]])
                    last_tok = S.emit("sp", (lambda dst=dst, src=src: (lambda e: e.dma_start(out=dst, in_=src)))(),
                                      reads=rd, writes=[B(f"hs_{ib}_{j}")], dma_sem=sem_hs)
            for ib in range(4):
                for j in range(4):
                    B(f"hs_{ib}_{j}").writer = last_tok

        def stage_Epe(i):
            fns = []
            for m in range(NGRP):
                for blk in range(16):
                    q, cj = blk // 4, blk % 4
                    fns.append((lambda m=m, blk=blk, q=q, cj=cj: (lambda e: e.matmul(
                        out=banks[q][32 * cj:32 * cj + 32, 0:T],
                        lhsT=Wd[:, blk * NGRP + m, :],
                        rhs=hs[:, cj * 4 + q, 4 * m:4 * m + T],
                        start=(m == 0), stop=(m == NGRP - 1), tile_position=(0, 32 * cj))))())
            rd = [B("Wd")] + [B(f"hs_{ib}_{j}") for ib in range(4) for j in range(4)]
            S.emit("pe", fns, reads=rd, writes=[bankB[b] for b in range(4)])

        def stage_Eevac(i):
            for b in range(4):
                S.emit("act", (lambda b=b: (lambda e: e.activation(out=cbf[:, b, :], in_=banks[b][:, 0:T], func=AF.Identity,
                                                                  bias=cols[:, C_CB + b:C_CB + b + 1])))(),
                       reads=[bankB[b], B("cols")], writes=[B(f"cbf{b}")])
            for b in range(4):
                S.emit("act", (lambda b=b: (lambda e: e.activation(out=csq[:, b, :], in_=banks[b][:, 0:T], func=AF.Square,
                                                                  bias=cols[:, C_CB + b:C_CB + b + 1])))(),
                       reads=[bankB[b], B("cols")], writes=[B(f"csq{b}")])

        def stage_Eevac32(i):
            for b in range(4):
                S.emit("act", (lambda b=b: (lambda e: e.activation(out=c32[:, b, :], in_=banks[b][:, 0:T], func=AF.Identity,
                                                                  bias=cols[:, C_CB + b:C_CB + b + 1])))(),
                       reads=[bankB[b], B("cols")], writes=[B(f"c32_{b}")])

        def stage_F1(i):
            bk_mu = next_bank()
            bk_sq = next_bank()
            fns = [(lambda iq=iq: (lambda e: e.matmul(out=banks[bk_mu][:, 0:T], lhsT=ones_bf[:, :], rhs=cbf[:, iq, :], start=(iq == 0), stop=(iq == 3))))()
                   for iq in range(4)]
            S.emit("pe", fns, reads=[B("ones_bf")] + [B(f"cbf{iq}") for iq in range(4)], writes=[bankB[bk_mu]])
            fns = [(lambda iq=iq: (lambda e: e.matmul(out=banks[bk_sq][:, 0:T], lhsT=ones_bf[:, :], rhs=csq[:, iq, :], start=(iq == 0), stop=(iq == 3))))()
                   for iq in range(4)]
            S.emit("pe", fns, reads=[B("ones_bf")] + [B(f"csq{iq}") for iq in range(4)], writes=[bankB[bk_sq]])
            S.emit("act", lambda e: e.activation(out=mu_sb[:, :], in_=banks[bk_mu][:, 0:T], func=AF.Copy),
                   reads=[bankB[bk_mu]], writes=[B("mu_sb")])
            S.emit("act", lambda e: e.activation(out=t1[:, :], in_=banks[bk_mu][:, 0:T], func=AF.Square),
                   reads=[bankB[bk_mu]], writes=[B("t1")])
            S.emit("dve", lambda e: e.scalar_tensor_tensor(out=vareps[:, :], in0=banks[bk_sq][:, 0:T], scalar=LN_EPS, in1=t1[:, :],
                                                           op0=ALU.add, op1=ALU.subtract),
                   reads=[bankB[bk_sq], B("t1")], writes=[B("vareps")])
            S.emit("act", lambda e: e.activation(out=nw_a[:, :], in_=vareps[:, :], func=AF.Sqrt),
                   reads=[B("vareps")], writes=[B("nw_a")])

        def stage_F1b(i):
            S.emit("dve", lambda e: e.reciprocal(out=rstd_ln[:, :], in_=nw_a[:, :]),
                   reads=[B("nw_a")], writes=[B("rstd_ln")])
            S.emit("dve", lambda e: e.scalar_tensor_tensor(out=nmr[:, :], in0=mu_sb[:, :], scalar=-1.0, in1=rstd_ln[:, :],
                                                           op0=ALU.mult, op1=ALU.mult),
                   reads=[B("mu_sb"), B("rstd_ln")], writes=[B("nmr")])

        def stage_F2dve(i, pair):
            cp = c32[:, 2 * pair:2 * pair + 2, :]
            pst = list(rstd_ln[:, :].ap[0])
            r_b = bass.AP(rstd_ln, 0, [pst, [0, 2], [1, T]])
            n_b = bass.AP(nmr, 0, [list(nmr[:, :].ap[0]), [0, 2], [1, T]])
            bufs = [B(f"c32_{2 * pair}"), B(f"c32_{2 * pair + 1}")]
            S.emit("dve", lambda e: e.tensor_tensor(out=cp, in0=cp, in1=r_b, op=ALU.mult), reads=bufs + [B("rstd_ln")], writes=bufs)
            S.emit("dve", lambda e: e.tensor_tensor(out=cp, in0=cp, in1=n_b, op=ALU.add), reads=bufs + [B("nmr")], writes=bufs)

        def stage_F2act(i, iqs):
            for iq in iqs:
                S.emit("act", (lambda iq=iq: (lambda e: e.activation(out=s_bf[:, iq, :], in_=c32[:, iq, :], func=AF.Silu,
                                                                    scale=cols[:, C_LG + iq:C_LG + iq + 1], bias=cols[:, C_LB + iq:C_LB + iq + 1])))(),
                       reads=[B(f"c32_{iq}"), B("cols")], writes=[B(f"s_bf{iq}")])

        nw_a = sb("nw_a", [128, T], F32)
        def stage_G(i, fixed_banks=None):
            par = i % 2
            for o in range(4):
                bk = next_bank() if fixed_banks is None else fixed_banks[o]
                fns = [(lambda iq=iq, o=o, bk=bk: (lambda e: e.matmul(out=banks[bk][:, 0:T], lhsT=pw_bf[:, iq, o * 128:(o + 1) * 128], rhs=s_bf[:, iq, :],
                                                                     start=(iq == 0), stop=(iq == 3))))() for iq in range(4)]
                S.emit("pe", fns, reads=[B("pww")] + [B(f"s_bf{iq}") for iq in range(4)], writes=[bankB[bk]])
                S.emit("dve", (lambda o=o, bk=bk: (lambda e: e.scalar_tensor_tensor(out=yT[par][:, 4 + o, :], in0=banks[bk][:, 0:T],
                                                                                  scalar=cols[:, C_PWB + o:C_PWB + o + 1], in1=gb[par][:, o, :],
                                                                                  op0=ALU.add, op1=ALU.mult)))(),
                       reads=[bankB[bk], B(f"gb{par}_{o}"), B("cols")], writes=[B(f"yT{par}_{4 + o}")])

        h1_banks = {}

        def stage_H1(i, phase="both", fixed_banks=None):
            par = i % 2
            slot = i % NXB
            for j in range(2):
                for hf in range(2):
                    if phase in ("both", "first"):
                        bk = next_bank() if fixed_banks is None else fixed_banks[2 * j + hf]
                        h1_banks[(i, j, hf)] = bk
                    else:
                        bk = h1_banks[(i, j, hf)]
                    fns = [(lambda e_=e_, j=j, hf=hf, bk=bk: (lambda e: e.matmul(out=banks[bk][:, :], lhsT=yT[par][:, e_, j * 128:(j + 1) * 128],
                                                                                rhs=w_out_bf[:, e_, hf * 512:(hf + 1) * 512],
                                                                                start=(e_ == 0), stop=(e_ == 7))))() for e_ in range(8)]
                    if phase in ("both", "first"):
                        S.emit("pe", fns[:4], reads=[B("wout")] + [B(f"yT{par}_{e_}") for e_ in range(4)], writes=[bankB[bk]])
                    if phase in ("both", "second"):
                        S.emit("pe", fns[4:], reads=[B("wout")] + [B(f"yT{par}_{e_}") for e_ in range(4, 8)], writes=[bankB[bk]])
                        S.emit("dve", (lambda j=j, hf=hf, bk=bk: (lambda e: e.tensor_tensor(out=xb[slot][:, j, hf * 512:(hf + 1) * 512],
                                                                                           in0=banks[bk][:, :], in1=xb[slot][:, j, hf * 512:(hf + 1) * 512],
                                                                                           op=ALU.add)))(),
                               reads=[bankB[bk], B(f"xb{slot}_{j}_{hf}")], writes=[B(f"xb{slot}_{j}_{hf}")])

        def stage_H2(i):
            slot = i % NXB
            for j in range(2):
                S.emit("act", (lambda j=j: (lambda e: e.activation(out=junk[:, :], in_=xb[slot][:, j, :], func=AF.Square,
                                                                  accum_out=ssq2[:, 2 * i + j:2 * i + j + 1])))(),
                       reads=[B(f"xb{slot}_{j}_0"), B(f"xb{slot}_{j}_1")], writes=[B("junk"), B(f"ssq2{2 * i}")])
            rsqrt_small(ssq2, ms2, rstd2, 2 * i, RMS_EPS, 1.0 / D, "ssq2", "ms2", "rstd2")
            for j in range(2):
                S.emit("dve", (lambda j=j: (lambda e: e.scalar_tensor_tensor(out=xb[slot][:, j, :], in0=xb[slot][:, j, :],
                                                                            scalar=rstd2[:, 2 * i + j:2 * i + j + 1], in1=gfin[:, :],
                                                                            op0=ALU.mult, op1=ALU.mult)))(),
                       reads=[B(f"xb{slot}_{j}_0"), B(f"xb{slot}_{j}_1"), B(f"rstd2{2 * i}"), B("gfin")],
                       writes=[B(f"xb{slot}_{j}_0"), B(f"xb{slot}_{j}_1")])
            return S.emit("sp", lambda e: e.dma_start(out=out_t[i], in_=xb[slot][:, :, :]), reads=xbufs(slot), dma_sem=sem_xs[slot])

        def stage_H2_split(i):
            slot = i % NXB
            toks = []
            for j in range(2):
                c = 2 * i + j
                S.emit("act", (lambda j=j, c=c: (lambda e: e.activation(out=junk[:, :], in_=xb[slot][:, j, :], func=AF.Square,
                                                                       accum_out=ssq2[:, c:c + 1])))(),
                       reads=[B(f"xb{slot}_{j}_0"), B(f"xb{slot}_{j}_1")], writes=[B("junk"), B(f"ssq2s{c}")])
            for j in range(2):
                c = 2 * i + j
                S.emit("pool", (lambda c=c: (lambda e: e.tensor_scalar(out=ms2[:, c:c + 1], in0=ssq2[:, c:c + 1], scalar1=1.0 / D, scalar2=RMS_EPS,
                                                                      op0=ALU.mult, op1=ALU.add)))(),
                       reads=[B(f"ssq2s{c}")], writes=[B(f"ms2s{c}")])
                S.emit("pool", (lambda c=c: (lambda e: e.tensor_tensor(out=rstd2[:, c:c + 1], in0=ms2[:, c:c + 1], in1=neghalf[:, 0:1], op=ALU.pow)))(),
                       reads=[B(f"ms2s{c}"), B("neghalf")], writes=[B(f"rstd2s{c}")])
                if j == 0:
                    S.emit("dve", (lambda j=j, c=c: (lambda e: e.scalar_tensor_tensor(out=xb[slot][:, j, :], in0=xb[slot][:, j, :],
                                                                                     scalar=rstd2[:, c:c + 1], in1=gfin[:, :],
                                                                                     op0=ALU.mult, op1=ALU.mult)))(),
                           reads=[B(f"xb{slot}_{j}_0"), B(f"xb{slot}_{j}_1"), B(f"rstd2s{c}"), B("gfin")],
                           writes=[B(f"xb{slot}_{j}_0"), B(f"xb{slot}_{j}_1")])
                    toks.append(S.emit("sp", (lambda j=j: (lambda e: e.dma_start(out=out_t[i][:, j, :], in_=xb[slot][:, j, :])))(),
                                       reads=[B(f"xb{slot}_{j}_0"), B(f"xb{slot}_{j}_1")], dma_sem=sem_xs[slot]))
                else:
                    for hf in range(2):
                        cs_ = slice(hf * 512, (hf + 1) * 512)
                        S.emit("dve", (lambda j=j, c=c, cs_=cs_: (lambda e: e.scalar_tensor_tensor(out=xb[slot][:, j, cs_], in0=xb[slot][:, j, cs_],
                                                                                                  scalar=rstd2[:, c:c + 1], in1=gfin[:, cs_],
                                                                                                  op0=ALU.mult, op1=ALU.mult)))(),
                               reads=[B(f"xb{slot}_{j}_{hf}"), B(f"rstd2s{c}"), B("gfin")], writes=[B(f"xb{slot}_{j}_{hf}")])
                        toks.append(S.emit("sp", (lambda j=j, cs_=cs_: (lambda e: e.dma_start(out=out_t[i][:, j, cs_], in_=xb[slot][:, j, cs_])))(),
                                           reads=[B(f"xb{slot}_{j}_{hf}")], dma_sem=sem_xs[slot]))
            return toks[-1]

        store_toks = []
        ntiles = NT if debug is None else debug.get("_ntiles", NT)
        def dumps(i):
            par = i % 2
            dbg("xT", xT[par][:, :, :], [128, 8, T], BF16, [B(f"xT{par}_0"), B(f"xT{par}_1")])
            dbg("ub", ub[par][:, :, :], [128, 4, UH + T + 1], F32, [B(f"um{par}_{g}") for g in range(4)])
            dbg("hb", hb[par][:, :, :], [128, 4, HH + T + 2], BF16, [B(f"hm{par}_{g}") for g in range(4)])
            dbg("ga", ga[:, :, :], [128, 4, T], BF16, [B(f"ga{g}") for g in range(4)])
            dbg("gb", gb[par][:, :, :], [128, 4, T], BF16, [B(f"gb{par}_{g}") for g in range(4)])
            dbg("sbf", sbf[:, :, :], [128, 4, T], BF16, [B(f"sbf{g}") for g in range(4)])
            dbg("cbf", cbf[:, :, :], [128, 4, T], BF16, [B(f"cbf{g}") for g in range(4)])
            dbg("rstd_ln", rstd_ln[:, :], [128, T], F32, [B("rstd_ln")])
            dbg("mu_sb", mu_sb[:, :], [128, T], F32, [B("mu_sb")])
            dbg("s_bf", s_bf[:, :, :], [128, 4, T], BF16, [B(f"s_bf{g}") for g in range(4)])
            dbg("yT", yT[par][:, :, :], [128, 8, T], BF16, [B(f"yT{par}_{g}") for g in range(8)])
            dbg("rstdx", rstdx[:, :], [128, 2 * NT], F32, [B(f"rstdx{2 * i}")])

        def tile0_pre():
            stage_Aload(0)
            stage_Acomp(0)
        emit_win("A", hook3=tile0_pre)
        stage_B(0)
        emit_win("B")
        if ntiles > 1:
            stage_Aload(1)
        stage_C(0, ("glu",))
        emit_win("C_dma")
        wide_stage[0] = False
        stage_i[0] = 0
        if ntiles > 1:
            stage_Acomp_a(1)
        stage_C(0, ("bval",))
        stage_stack(0)
        if ntiles > 1:
            stage_Acomp_b(1)
        emit_win("C_cast")
        emit_poolpw_dma(3, 4)
        emit_wout_dma(0, 2)
        emit_gfin()
        stage_C(0, ("aval",))
        stage_Dpool(0)
        stage_C(0, ("agate",))
        stage_Dmm(0, main=False)
        emit_poolpw_cast(3, 4)
        hs_done = [B(f"hs_{ib}_{j}") for ib in range(4) for j in range(4)]
        emit_wout_dma(1, 3, after=hs_done)
        emit_wout_dma(2, 4, after=hs_done)
        stage_C(0, ("bgate",))
        if ntiles > 1:
            stage_B(1)
        emit_wout_cast(0)
        if ntiles > 2:
            stage_Aload(2)
        stage_Epe(0)
        stage_Eevac(0)
        stage_Dmm(0, pre=False)
        if ntiles == 1:
            emit_wout_cast(1)
            emit_wout_cast(2)
            emit_wout_dma(3, 3)
            emit_wout_cast(3)
        for i in range(ntiles):
            n1 = i + 1 < ntiles
            n2 = i + 2 < ntiles
            if n1:
                stage_C(i + 1, ("glu",))
            if i == 0 and ntiles > 1:
                emit_wout_cast(1, eng="dve")
                emit_wout_dma(3, 3)
            stage_F1(i)
            stage_Eevac32(i)
            if ntiles >= 3 and i == ntiles - 1:
                stage_G(i - 1)
            if n2:
                stage_Acomp_a(i + 2)
            if n1:
                stage_C(i + 1, ("bval",))
                stage_stack(i + 1)
            stage_F1b(i)
            if i == 0 and ntiles > 1:
                emit_wout_cast(2, eng="dve")
                emit_wout_cast(3, eng="dve")
            if n2:
                stage_Acomp_b(i + 2)
            if n1:
                stage_C(i + 1, ("aval",))
                stage_Dpool(i + 1)
            if n1:
                stage_C(i + 1, ("agate",))
            stage_F2dve(i, 0)
            stage_F2dve(i, 1)
            stage_F2act(i, (0, 1))
            if n1:
                stage_C(i + 1, ("bgate",), qs=(0, 1))
            stage_F2act(i, (2,))
            if n1:
                stage_C(i + 1, ("bgate",), qs=(2, 3))
            stage_F2act(i, (3,))
            if n2:
                stage_B(i + 2)
            last_fill = (ntiles >= 3 and i == ntiles - 1)
            if last_fill:
                stage_H1(i - 1)
                stage_H1(i, phase="first", fixed_banks=(0, 1, 2, 3))
                stage_G(i)
            elif not (ntiles >= 3 and i == ntiles - 2):
                stage_G(i)
            if i > 0:
                store_toks.append(stage_H2(i - 1))
            if i + 3 < ntiles:
                stage_Aload(i + 3)
            defer_h1 = (ntiles >= 3 and i == ntiles - 2)
            if last_fill:
                stage_H1(i, phase="second")
            elif not defer_h1:
                stage_H1(i)
            if n1:
                stage_Epe(i + 1)
                stage_Eevac(i + 1)
                stage_Dmm(i + 1)
            if debug is not None and i == debug.get("_tile", 0):
                dumps(i)
        store_toks.append(stage_H2_split(ntiles - 1))
        final = {}
        for sem, val in store_toks:
            final[sem] = max(final.get(sem, 0), val)
        if sem_dbg.count:
            final[sem_dbg] = sem_dbg.count
        S.wait_only("sp", list(final.items()))

        with nc.Block() as block:
            @block.sync
            def _(sync):
                S.replay("sp", sync)

            @block.tensor
            def _(tensor):
                S.replay("pe", tensor)

            @block.scalar
            def _(scalar):
                S.replay("act", scalar)

            @block.vector
            def _(vector):
                S.replay("dve", vector)

            @block.gpsimd
            def _(gpsimd):
                S.replay("pool", gpsimd)

    return nc, dbg_out


def _perm_cols(v):
    return np.ascontiguousarray(v.reshape(4, 128).T)


def prepare_inputs(x, norm_g, w_in, pool_w, pool_b, pool_scale, conv_dw, conv_b,
                   ln_g, ln_b, pw_w, pw_b, w_out, final_g):
    f = lambda a: np.ascontiguousarray(np.asarray(a, dtype=np.float32))
    x = f(x)
    cols = np.zeros((128, NCOLS), np.float32)
    cols[:, C_NG:C_NG + 8] = f(norm_g).reshape(8, 128).T
    cols[:, C_PS:C_PS + 4] = f(pool_scale).reshape(4, 128).T
    cols[:, C_PB:C_PB + 4] = f(pool_b).reshape(4, 128).T
    cols[:, C_CB:C_CB + 4] = _perm_cols(f(conv_b))
    cols[:, C_LG:C_LG + 4] = _perm_cols(f(ln_g))
    cols[:, C_LB:C_LB + 4] = _perm_cols(f(ln_b))
    cols[:, C_PWB:C_PWB + 4] = f(pw_b).reshape(4, 128).T
    cw4 = np.zeros((NGRP * 4, 512), np.float32)
    cw4[:KW] = f(conv_dw)
    cols[:, C_CW:] = cw4.reshape(NGRP, 4, 16, 32).transpose(1, 3, 2, 0).reshape(128, 16 * NGRP)
    pww = f(pw_w).reshape(4, 128, 512).transpose(1, 0, 2).reshape(128, 4 * 512)
    plw = f(pool_w).transpose(1, 0, 2).reshape(128, 4 * 128)
    gfin = np.broadcast_to(f(final_g)[None, :], (128, D))
    shared = {
        "w_in": f(w_in), "w_out": f(w_out), "pw_w": np.ascontiguousarray(pww),
        "pool_w": np.ascontiguousarray(plw), "cols": cols, "gfin": np.ascontiguousarray(gfin),
    }
    xs = x.reshape(NCORES, TOK_PER_CORE, D)
    in_maps = []
    for c in range(NCORES):
        m = dict(shared)
        m["x"] = np.ascontiguousarray(xs[c])
        in_maps.append(m)
    return in_maps


def kernel(x, norm_g, w_in, pool_w, pool_b, pool_scale, conv_dw, conv_b,
           ln_g, ln_b, pw_w, pw_b, w_out, final_g):
    in_maps = prepare_inputs(x, norm_g, w_in, pool_w, pool_b, pool_scale, conv_dw, conv_b,
                             ln_g, ln_b, pw_w, pw_b, w_out, final_g)
    nc, _ = build_nc()
    res = run_bass_kernel_spmd(nc, in_maps, core_ids=list(range(NCORES)))
    out = np.stack([np.asarray(r["out"], dtype=np.float32) for r in res.results], axis=0)
    return out.reshape(16, SEQ, D)
```

```python
import contextlib
import numpy as np
import concourse.bass as bass
import concourse.mybir as mybir
from concourse.bass_utils import run_bass_kernel_spmd

F32 = mybir.dt.float32
BF16 = mybir.dt.bfloat16
I32 = mybir.dt.int32
AF = mybir.ActivationFunctionType
ALU = mybir.AluOpType

NCORES = 8
D = 1024
DIN = 2560
SEQ = 2048
TOK_PER_CORE = 2 * SEQ
T = 256
NT = TOK_PER_CORE // T
TPS = SEQ // T
KW = 31
POOL_W = (2, 4, 8, 16)
UH = 15
HH = 30
RMS_EPS = 1e-6
LN_EPS = 1e-5

C_NG, C_PS, C_PB, C_CB, C_LG, C_LB, C_PWB, C_CW = 0, 8, 12, 16, 20, 24, 28, 32
NGRP = 8
NCOLS = C_CW + 16 * NGRP
LS = T + 28


class Sem:
    def __init__(self, h, name):
        self.h = h
        self.name = name
        self.count = 0


class Buf:
    __slots__ = ("name", "writer", "readers")

    def __init__(self, name):
        self.name = name
        self.writer = None
        self.readers = {}


class Eng:
    def __init__(self, name, sem):
        self.name = name
        self.sem = sem
        self.ops = []
        self.waited = {}


class Sched:
    def __init__(self, nc, stack):
        self.nc = nc
        self.stack = stack
        self.engs = {}
        for n in ("pe", "act", "dve", "pool", "sp"):
            self.engs[n] = Eng(n, self.new_sem("e_" + n))
        self.bufs = {}

    def new_sem(self, name):
        return Sem(self.stack.enter_context(self.nc.semaphore(name)), name)

    def buf(self, name):
        b = self.bufs.get(name)
        if b is None:
            b = Buf(name)
            self.bufs[name] = b
        return b

    def emit(self, eng, fns, reads=(), writes=(), dma_sem=None):
        e = self.engs[eng]
        if not isinstance(fns, (list, tuple)):
            fns = [fns]
        waits = {}

        def need(tok):
            if tok is None:
                return
            sem, val = tok
            if eng == "pe" and sem is e.sem:
                return
            if e.waited.get(sem, 0) >= val:
                return
            if waits.get(sem, 0) < val:
                waits[sem] = val

        for b in reads:
            need(b.writer)
        for b in writes:
            need(b.writer)
            for s, v in b.readers.items():
                need((s, v))
        for s, v in waits.items():
            e.waited[s] = v
        if dma_sem is None:
            e.sem.count += 1
            tok = (e.sem, e.sem.count)
            amt = 1
        else:
            dma_sem.count += 16
            tok = (dma_sem, dma_sem.count)
            amt = 16
        e.ops.append((list(waits.items()), list(fns), tok[0], amt))
        for b in reads:
            if b.readers.get(tok[0], 0) < tok[1]:
                b.readers[tok[0]] = tok[1]
        for b in writes:
            b.writer = tok
            b.readers = {}
        return tok

    def wait_only(self, eng, toks):
        e = self.engs[eng]
        waits = {}
        for sem, val in toks:
            if e.waited.get(sem, 0) >= val:
                continue
            if waits.get(sem, 0) < val:
                waits[sem] = val
        for s, v in waits.items():
            e.waited[s] = v
        e.ops.append((list(waits.items()), [], None, 0))

    def replay(self, eng, handle):
        for waits, fns, sem, amt in self.engs[eng].ops:
            for s, v in waits:
                handle.wait_ge(s.h, v)
            ins = None
            for f in fns:
                ins = f(handle)
            if ins is not None and sem is not None:
                ins.then_inc(sem.h, amt)


def build_nc(debug=None):
    nc = bass.Bass("TRN2", target_bir_lowering=False)
    dram = {}

    def din(name, shape):
        dram[name] = nc.dram_tensor(name, shape, F32, kind="ExternalInput")
        return dram[name]

    x_d = din("x", [TOK_PER_CORE, D])
    win_d = din("w_in", [D, DIN])
    wout_d = din("w_out", [D, D])
    pww_d = din("pw_w", [128, 4 * 512])
    plw_d = din("pool_w", [128, 4 * 128])
    cols_d = din("cols", [128, NCOLS])
    gfin_d = din("gfin", [128, D])
    out_d = nc.dram_tensor("out", [TOK_PER_CORE, D], F32, kind="ExternalOutput")
    x_t = x_d.ap().rearrange("(n j p) d -> n p j d", p=128, j=2)
    out_t = out_d.ap().rearrange("(n j p) d -> n p j d", p=128, j=2)

    dbg_out = {}

    with contextlib.ExitStack() as st:
        S = Sched(nc, st)

        def sb(name, shape, dt):
            return st.enter_context(nc.sbuf_tensor("sb_" + name, shape, dt))

        cols = sb("cols", [128, NCOLS], F32)
        gfin = sb("gfin", [128, D], F32)
        ident_f = sb("ident_f", [128, 128], F32)
        ident_bf = sb("ident_bf", [128, 128], BF16)
        ones_bf = sb("ones_bf", [128, 128], BF16)
        identblk = sb("identblk", [128, 32], F32)
        cwh = sb("cwh", [128, 16 * NGRP], F32)
        Wd = sb("Wd", [128, 16 * NGRP, 32], BF16)
        hs = sb("hs", [128, 16, HH + T + 2], BF16)
        fac = sb("fac", [128, 4, 16], F32)
        neghalf = sb("neghalf", [128, 8], F32)
        w_in_bf = sb("w_in_bf", [128, 8, DIN], BF16)
        w_out_bf = sb("w_out_bf", [128, 8, D], BF16)
        pw_bf = sb("pw_bf", [128, 4, 512], BF16)
        pwa_bf = sb("pwa_bf", [128, 4, 128], BF16)
        NXB = 5
        xb = [sb(f"xb{i}", [128, 2, D], F32) for i in range(NXB)]
        xn = [sb(f"xn{i}", [128, 2, D], BF16) for i in range(2)]
        xT = [sb(f"xT{i}", [128, 8, T], BF16) for i in range(2)]
        ub = [sb(f"ub{i}", [128, 4, UH + T + 1], F32) for i in range(2)]
        sA = sb("sA", [128, UH + T + 1], F32)
        sB = sb("sB", [128, UH + T + 1], F32)
        sbf = sb("sbf", [128, 4, T], BF16)
        sS = sb("sS", [128, 4, T], F32)
        ga = sb("ga", [128, 4, T], BF16)
        gb = [sb(f"gb{i}", [128, 4, T], BF16) for i in range(2)]
        th = sb("th", [128, 4, T], F32)
        hb = [sb(f"hb{i}", [128, 4, HH + T + 2], BF16) for i in range(2)]
        c32 = sb("c32", [128, 4, T], F32)
        cbf = sb("cbf", [128, 4, T], BF16)
        csq = sb("csq", [128, 4, T], BF16)
        mu_sb = sb("mu_sb", [128, T], F32)
        t1 = sb("t1", [128, T], F32)
        vareps = sb("vareps", [128, T], F32)
        rstd_ln = sb("rstd_ln", [128, T], F32)
        nmr = sb("nmr", [128, T], F32)
        s_bf = sb("s_bf", [128, 4, T], BF16)
        yT = [sb(f"yT{i}", [128, 8, T], BF16) for i in range(2)]
        ssq = sb("ssq", [128, 2 * NT], F32)
        msx = sb("msx", [128, 2 * NT], F32)
        rstdx = sb("rstdx", [128, 2 * NT], F32)
        ssq2 = sb("ssq2", [128, 2 * NT], F32)
        ms2 = sb("ms2", [128, 2 * NT], F32)
        rstd2 = sb("rstd2", [128, 2 * NT], F32)
        junk = sb("junk", [128, D], BF16)

        banks = [st.enter_context(nc.psum_tensor(f"bank{i}", [128, 512], F32)) for i in range(8)]
        bankB = [S.buf(f"bank{i}") for i in range(8)]
        gen_state = {"next": 0}
        GEN_BANKS = (4, 5, 6, 7)

        def next_bank():
            b = GEN_BANKS[gen_state["next"] % len(GEN_BANKS)]
            gen_state["next"] += 1
            return b

        B = S.buf
        sem_small = S.new_sem("d_small")
        sem_small2 = S.new_sem("d_small2")
        sem_hs = S.new_sem("d_hs")
        sem_xl = [S.new_sem(f"d_xl{i}") for i in range(NXB)]
        sem_xs = [S.new_sem(f"d_xs{i}") for i in range(NXB)]
        sem_dbg = S.new_sem("d_dbg")

        def xbufs(slot):
            return [B(f"xb{slot}_{j}_{h}") for j in range(2) for h in range(2)]

        def dbg(name, tensor, shape, dt, reads):
            if debug is None or name not in debug:
                return
            dd = nc.dram_tensor("dbg_" + name, shape, dt, kind="ExternalOutput")
            dbg_out[name] = dd
            S.emit("sp", lambda e: e.dma_start(out=dd.ap(), in_=tensor), reads=reads, dma_sem=sem_dbg)

        S.emit("sp", lambda e: e.dma_start(out=cols[:, :], in_=cols_d.ap()), writes=[B("cols")], dma_sem=sem_small)
        def emit_gfin():
            S.emit("sp", lambda e: e.dma_start(out=gfin[:, :], in_=gfin_d.ap()), writes=[B("gfin")], dma_sem=sem_small2)

        S.emit("pool", lambda e: e.memset(ident_f[:, :], 0.0), writes=[B("ident_f")])
        S.emit("pool", lambda e: e.affine_select(out=ident_f[:, :], in_=ident_f[:, :], pattern=[[-1, 128]],
                                                  compare_op=ALU.not_equal, fill=1.0, base=0, channel_multiplier=1),
               reads=[B("ident_f")], writes=[B("ident_f")])
        S.emit("pool", lambda e: e.tensor_copy(out=ident_bf[:, :], in_=ident_f[:, :]), reads=[B("ident_f")], writes=[B("ident_bf")])
        S.emit("pool", lambda e: e.memset(ones_bf[:, :], 1.0 / 512.0), writes=[B("ones_bf")])
        S.emit("pool", lambda e: e.memset(neghalf[:, :], -0.5), writes=[B("neghalf")])
        S.emit("pool", lambda e: e.tensor_tensor(out=identblk[:, :], in0=ident_f[:, 0:32], in1=ident_f[:, 32:64], op=ALU.add),
               reads=[B("ident_f")], writes=[B("identblk")])
        S.emit("pool", lambda e: e.tensor_tensor(out=identblk[:, :], in0=identblk[:, :], in1=ident_f[:, 64:96], op=ALU.add),
               reads=[B("ident_f"), B("identblk")], writes=[B("identblk")])
        S.emit("pool", lambda e: e.tensor_tensor(out=identblk[:, :], in0=identblk[:, :], in1=ident_f[:, 96:128], op=ALU.add),
               reads=[B("ident_f"), B("identblk")], writes=[B("identblk")])
        for par_ in range(2):
            S.emit("pool", (lambda par_=par_: (lambda e: e.memset(hb[par_][:, :, HH + T:HH + T + 2], 0.0)))(), writes=[B(f"hpad{par_}")])
        S.emit("pool", lambda e: e.memset(fac[:, :, :], 1.0), writes=[B("fac")])
        for g, w in enumerate(POOL_W):
            for t in range(w - 1):
                S.emit("pool", (lambda g=g, t=t, w=w: (lambda e: e.memset(fac[:, g, t:t + 1], float(w) / float(t + 1))))(),
                       reads=[B("fac")], writes=[B("fac")])
        S.emit("pool", lambda e: e.tensor_scalar(out=cwh[:, :], in0=cols[:, C_CW:C_CW + 16 * NGRP], scalar1=0.5, scalar2=0.0, op0=ALU.mult, op1=ALU.add),
               reads=[B("cols")], writes=[B("cwh")])
        ib_ap = identblk[:, :]
        cw_ap = cwh[:, :]
        ib_b = bass.AP(identblk, 0, [list(ib_ap.ap[0]), [0, 16 * NGRP], [1, 32]])
        cw_b = bass.AP(cwh, 0, [list(cw_ap.ap[0]), [1, 16 * NGRP], [0, 32]])
        S.emit("pool", lambda e: e.tensor_tensor(out=Wd[:, :, :], in0=ib_b, in1=cw_b, op=ALU.mult),
               reads=[B("identblk"), B("cwh")], writes=[B("Wd")])

        stage_slots = [3, 4]
        stage_sem = {2: S.new_sem("d_st2"), 3: S.new_sem("d_st3"), 4: S.new_sem("d_st4")}
        stage_i = [0]
        cast_i = [0]

        wide_stage = [True]

        def next_stage():
            lst = (2, 3, 4) if wide_stage[0] else (3, 4)
            slot = lst[stage_i[0] % len(lst)]
            stage_i[0] += 1
            return slot

        def cast_op(dst_ap, src_ap, scale, rd, wr, eng=None):
            if eng is None:
                eng = ("dve", "act", "dve")[cast_i[0] % 3]
                cast_i[0] += 1
            if eng == "act":
                if scale is None:
                    fn = lambda e: e.activation(out=dst_ap, in_=src_ap, func=AF.Copy)
                else:
                    fn = lambda e: e.activation(out=dst_ap, in_=src_ap, func=AF.Copy, scale=scale)
            else:
                if scale is None:
                    fn = lambda e: e.tensor_copy(out=dst_ap, in_=src_ap)
                else:
                    fn = lambda e: e.tensor_scalar(out=dst_ap, in0=src_ap, scalar1=scale, scalar2=None, op0=ALU.mult)
            S.emit(eng, fn, reads=rd, writes=wr)

        winC_slot = {}

        def emit_win(blk, hook3=None):
            pst = list(xb[0][:, :, :].ap[0])
            if blk in ("A", "B"):
                c0 = 1024 if blk == "A" else 0
                blks = (2, 3) if blk == "A" else (0, 1)
                for pr in range(4):
                    if pr == 3 and hook3 is not None:
                        hook3()
                    slot = next_stage()
                    sv = xb[slot][:, :, :]
                    src = win_d.ap()[pr * 256:(pr + 1) * 256, c0:c0 + 1024].rearrange("(c p) n -> p c n", p=128)
                    S.emit("sp", (lambda sv=sv, src=src: (lambda e: e.dma_start(out=sv, in_=src)))(), writes=xbufs(slot), dma_sem=stage_sem[slot])
                    for cl in range(2):
                        c = 2 * pr + cl
                        cast_op(w_in_bf[:, c, c0:c0 + 1024], xb[slot][:, cl, :], cols[:, C_NG + c:C_NG + c + 1],
                                xbufs(slot) + [B("cols")], [B(f"win{blks[0]}_{pr // 2}"), B(f"win{blks[1]}_{pr // 2}")])
            elif blk == "C_dma":
                for half in range(2):
                    slot = next_stage()
                    winC_slot[half] = slot
                    sv = bass.AP(xb[slot], 0, [pst, [512, 4], [1, 512]])
                    src = win_d.ap()[half * 512:(half + 1) * 512, 2048:2560].rearrange("(c p) n -> p c n", p=128)
                    S.emit("sp", (lambda sv=sv, src=src: (lambda e: e.dma_start(out=sv, in_=src)))(), writes=xbufs(slot), dma_sem=stage_sem[slot])
            else:
                for half in range(2):
                    slot = winC_slot[half]
                    sv = bass.AP(xb[slot], 0, [pst, [512, 4], [1, 512]])
                    for cl in range(4):
                        c = half * 4 + cl
                        cast_op(w_in_bf[:, c, 2048:2560], sv[:, cl, :], cols[:, C_NG + c:C_NG + c + 1],
                                xbufs(slot) + [B("cols")], [B(f"win4_{half}")])

        def emit_poolpw_dma(slot_pl, slot_pw):
            pst = list(xb[0][:, :, :].ap[0])
            sv_pl = bass.AP(xb[slot_pl], 0, [pst, [1, 512]])
            S.emit("sp", lambda e: e.dma_start(out=sv_pl, in_=plw_d.ap()), writes=xbufs(slot_pl), dma_sem=stage_sem[slot_pl])
            sv = bass.AP(xb[slot_pw], 0, [pst, [1, 2048]])
            S.emit("sp", (lambda sv=sv: (lambda e: e.dma_start(out=sv, in_=pww_d.ap())))(), writes=xbufs(slot_pw), dma_sem=stage_sem[slot_pw])

        def emit_poolpw_cast(slot_pl, slot_pw):
            pst = list(xb[0][:, :, :].ap[0])
            sv_pl = bass.AP(xb[slot_pl], 0, [pst, [1, 512]])
            for g, w in enumerate(POOL_W):
                S.emit("dve", (lambda g=g, w=w: (lambda e: e.tensor_scalar(out=pwa_bf[:, g, :], in0=sv_pl[:, g * 128:(g + 1) * 128],
                                                                          scalar1=1.0 / w, scalar2=None, op0=ALU.mult)))(),
                       reads=xbufs(slot_pl), writes=[B("wts")])
            sv = bass.AP(xb[slot_pw], 0, [pst, [1, 2048]])
            pw_flat = bass.AP(pw_bf, 0, [list(pw_bf[:, :, :].ap[0]), [1, 2048]])
            for hf in range(2):
                cast_op(pw_flat[:, hf * 1024:(hf + 1) * 1024], sv[:, hf * 1024:(hf + 1) * 1024], None, xbufs(slot_pw), [B("pww")],
                        eng="dve")

        wout_slot = {}

        def emit_wout_dma(m, slot, after=()):
            wout_slot[m] = slot
            sv = xb[slot][:, :, :]
            src = wout_d.ap()[m * 256:(m + 1) * 256, :].rearrange("(c p) n -> p c n", p=128)
            S.emit("sp", (lambda sv=sv, src=src: (lambda e: e.dma_start(out=sv, in_=src)))(), reads=list(after), writes=xbufs(slot), dma_sem=stage_sem[slot])

        def emit_wout_cast(m, eng=None):
            slot = wout_slot[m]
            for cl in range(2):
                c = 2 * m + cl
                sc = cols[:, C_PS + c:C_PS + c + 1] if c < 4 else None
                cast_op(w_out_bf[:, c, :], xb[slot][:, cl, :], sc, xbufs(slot) + [B("cols")], [B("wout")], eng=eng)

        def rsqrt_small(src_ssq, ms_t, out_t, c0, eps, scale, nsrc, nms, nout):
            S.emit("pool", lambda e: e.tensor_scalar(out=ms_t[:, c0:c0 + 2], in0=src_ssq[:, c0:c0 + 2], scalar1=scale, scalar2=eps,
                                                      op0=ALU.mult, op1=ALU.add),
                   reads=[B(f"{nsrc}{c0}")], writes=[B(f"{nms}{c0}")])
            S.emit("pool", lambda e: e.tensor_tensor(out=out_t[:, c0:c0 + 2], in0=ms_t[:, c0:c0 + 2], in1=neghalf[:, 0:2], op=ALU.pow),
                   reads=[B(f"{nms}{c0}"), B("neghalf")], writes=[B(f"{nout}{c0}")])

        def stage_Aload(i):
            slot = i % NXB
            S.emit("sp", lambda e: e.dma_start(out=xb[slot][:, :, :], in_=x_t[i]), writes=xbufs(slot), dma_sem=sem_xl[slot])

        def stage_Acomp_a(i):
            slot = i % NXB
            par = i % 2
            for j in range(2):
                S.emit("act", (lambda j=j: (lambda e: e.activation(out=xn[par][:, j, :], in_=xb[slot][:, j, :], func=AF.Square,
                                                                  accum_out=ssq[:, 2 * i + j:2 * i + j + 1])))(),
                       reads=[B(f"xb{slot}_{j}_0"), B(f"xb{slot}_{j}_1")], writes=[B(f"xn{par}_{j}"), B(f"ssq{2 * i}")])
            rsqrt_small(ssq, msx, rstdx, 2 * i, RMS_EPS, 1.0 / D, "ssq", "msx", "rstdx")

        def stage_Acomp_b(i):
            slot = i % NXB
            par = i % 2
            for j in range(2):
                S.emit("dve", (lambda j=j: (lambda e: e.tensor_scalar(out=xn[par][:, j, :], in0=xb[slot][:, j, :],
                                                                     scalar1=rstdx[:, 2 * i + j:2 * i + j + 1], scalar2=None, op0=ALU.mult)))(),
                       reads=[B(f"xb{slot}_{j}_0"), B(f"xb{slot}_{j}_1"), B(f"rstdx{2 * i}")], writes=[B(f"xn{par}_{j}")])

        def stage_Acomp(i):
            stage_Acomp_a(i)
            stage_Acomp_b(i)

        def stage_B(i):
            par = i % 2
            for grp in range(2):
                bk = 2 + grp
                pbf = banks[bk].bitcast(BF16)
                fns = []
                for dcl in range(4):
                    dc = grp * 4 + dcl
                    for j in range(2):
                        fns.append((lambda dc=dc, dcl=dcl, j=j, pbf=pbf: (lambda e: e.transpose(
                            out=pbf[:, dcl * T + j * 128: dcl * T + (j + 1) * 128],
                            in_=xn[par][:, j, dc * 128:(dc + 1) * 128], identity=ident_bf[:, :])))())
                S.emit("pe", fns, reads=[B(f"xn{par}_0"), B(f"xn{par}_1"), B("ident_bf")], writes=[bankB[bk]])
                dst = bass.AP(xT[par], grp * 4 * T, [list(xT[par][:, :, :].ap[0]), [1, 4 * T]])
                S.emit("act", (lambda pbf=pbf, dst=dst: (lambda e: e.activation(out=dst, in_=pbf[:, 0:4 * T], func=AF.Copy)))(),
                       reads=[bankB[bk]], writes=[B(f"xT{par}_{grp}")])

        def proj_group(i, f):
            par = i % 2
            bk = next_bank()
            fns = []
            for dc in range(8):
                fns.append((lambda dc=dc, bk=bk: (lambda e: e.matmul(out=banks[bk][:, 0:T], lhsT=w_in_bf[:, dc, f * 128:(f + 1) * 128],
                                                                   rhs=xT[par][:, dc, :], start=(dc == 0), stop=(dc == 7))))())
            S.emit("pe", fns, reads=[B(f"win{f // 4}_0"), B(f"win{f // 4}_1"), B(f"xT{par}_0"), B(f"xT{par}_1")], writes=[bankB[bk]])
            return bk

        def stage_C(i, which, qs=(0, 1, 2, 3)):
            par = i % 2
            first = (i % TPS == 0)
            for kind in which:
                for q in qs:
                    if kind == "glu":
                        bk = proj_group(i, 12 + q)
                        S.emit("act", (lambda q=q, bk=bk: (lambda e: e.activation(out=th[:, q, :], in_=banks[bk][:, 0:T], func=AF.Tanh, scale=0.5)))(),
                               reads=[bankB[bk]], writes=[B(f"th{q}")])
                    elif kind == "bval":
                        if q == 0:
                            for qq in range(4):
                                if first:
                                    S.emit("pool", (lambda qq=qq: (lambda e: e.memset(hb[par][:, qq, 0:HH], 0.0)))(),
                                           writes=[B(f"hh{par}_{qq}")])
                                else:
                                    S.emit("pool", (lambda qq=qq: (lambda e: e.tensor_copy(out=hb[par][:, qq, 0:HH], in_=hb[1 - par][:, qq, T:T + HH])))(),
                                           reads=[B(f"hm{1 - par}_{qq}")], writes=[B(f"hh{par}_{qq}")])
                        bk = proj_group(i, 8 + q)
                        S.emit("dve", (lambda q=q, bk=bk: (lambda e: e.scalar_tensor_tensor(out=hb[par][:, q, HH:HH + T], in0=th[:, q, :], scalar=1.0,
                                                                                          in1=banks[bk][:, 0:T], op0=ALU.add, op1=ALU.mult)))(),
                               reads=[bankB[bk], B(f"th{q}")], writes=[B(f"hm{par}_{q}")])
                    elif kind == "aval":
                        if q == 0:
                            for gg in range(4):
                                if first:
                                    S.emit("pool", (lambda gg=gg: (lambda e: e.memset(ub[par][:, gg, 0:UH], 0.0)))(),
                                           writes=[B(f"uh{par}_{gg}")])
                                else:
                                    S.emit("pool", (lambda gg=gg: (lambda e: e.tensor_copy(out=ub[par][:, gg, 0:UH], in_=ub[1 - par][:, gg, T:T + UH])))(),
                                           reads=[B(f"um{1 - par}_{gg}")], writes=[B(f"uh{par}_{gg}")])
                        bk = proj_group(i, 0 + q)
                        S.emit("act", (lambda q=q, bk=bk: (lambda e: e.activation(out=ub[par][:, q, UH:UH + T], in_=banks[bk][:, 0:T], func=AF.Copy)))(),
                               reads=[bankB[bk]], writes=[B(f"um{par}_{q}")])
                    elif kind == "agate":
                        bk = proj_group(i, 4 + q)
                        S.emit("act", (lambda q=q, bk=bk: (lambda e: e.activation(out=ga[:, q, :], in_=banks[bk][:, 0:T], func=AF.Silu)))(),
                               reads=[bankB[bk]], writes=[B(f"ga{q}")])
                    elif kind == "bgate":
                        bk = proj_group(i, 16 + q)
                        S.emit("act", (lambda q=q, bk=bk: (lambda e: e.activation(out=gb[par][:, q, :], in_=banks[bk][:, 0:T], func=AF.Silu)))(),
                               reads=[bankB[bk]], writes=[B(f"gb{par}_{q}")])

        def stage_Dpool(i):
            par = i % 2
            first = (i % TPS == 0)
            for g, w in enumerate(POOL_W):
                nlev = g + 1
                src = ub[par]
                rd = [B(f"um{par}_{g}"), B(f"uh{par}_{g}")]
                cur_lo = 0
                cur = None
                for l in range(nlev):
                    sh = 1 << l
                    lo = cur_lo + sh
                    last = (l == nlev - 1)
                    if last:
                        lo = UH
                    n = UH + T - lo
                    if cur is None:
                        in_a = src[:, g, lo:lo + n]
                        in_b = src[:, g, lo - sh:lo - sh + n]
                        rds = rd
                    else:
                        in_a = cur[:, lo:lo + n]
                        in_b = cur[:, lo - sh:lo - sh + n]
                        rds = [B("sA" if cur is sA else "sB")]
                    if last:
                        out_ap = sS[:, g, :]
                        wr = [B(f"sS{g}")]
                        nxt = None
                    else:
                        nxt = sA if (cur is not sA) else sB
                        out_ap = nxt[:, lo:lo + n]
                        wr = [B("sA" if nxt is sA else "sB")]
                    S.emit("pool", (lambda out_ap=out_ap, in_a=in_a, in_b=in_b: (lambda e: e.tensor_tensor(out=out_ap, in0=in_a, in1=in_b, op=ALU.add)))(),
                           reads=rds, writes=wr)
                    cur = nxt
                    cur_lo = lo
                if first:
                    S.emit("pool", (lambda g=g, w=w: (lambda e: e.tensor_tensor(out=sS[:, g, 0:w - 1], in0=sS[:, g, 0:w - 1], in1=fac[:, g, 0:w - 1], op=ALU.mult)))(),
                           reads=[B(f"sS{g}"), B("fac")], writes=[B(f"sS{g}")])

        def stage_Dmm(i, pre=True, main=True):
            par = i % 2
            for g, w in enumerate(POOL_W):
                if not pre:
                    break
                S.emit("dve", (lambda g=g, w=w: (lambda e: e.scalar_tensor_tensor(out=sbf[:, g, :], in0=ub[par][:, g, UH:UH + T], scalar=-float(w),
                                                                                 in1=sS[:, g, :], op0=ALU.mult, op1=ALU.add)))(),
                       reads=[B(f"um{par}_{g}"), B(f"sS{g}")], writes=[B(f"sbf{g}")])
            if not main:
                return
            bks = [next_bank(), next_bank()]
            fns = [(lambda g=g, bk=bks[g // 2]: (lambda e: e.matmul(out=banks[bk][:, (g % 2) * T:(g % 2 + 1) * T], lhsT=pwa_bf[:, g, :], rhs=sbf[:, g, :],
                                                                 start=True, stop=True)))()
                   for g in range(4)]
            S.emit("pe", fns, reads=[B("wts")] + [B(f"sbf{g}") for g in range(4)], writes=[bankB[b_] for b_ in bks])
            for g in range(4):
                bk = bks[g // 2]
                S.emit("dve", (lambda g=g, bk=bk: (lambda e: e.scalar_tensor_tensor(out=yT[par][:, g, :], in0=banks[bk][:, (g % 2) * T:(g % 2 + 1) * T],
                                                                                  scalar=cols[:, C_PB + g:C_PB + g + 1], in1=ga[:, g, :],
                                                                                  op0=ALU.add, op1=ALU.mult)))(),
                       reads=[bankB[bk], B(f"ga{g}"), B("cols")], writes=[B(f"yT{par}_{g}")])

        def stage_stack(i):
            par = i % 2
            pst = hs[:, :, :].ap[0][0]
            rd = [B(f"hm{par}_{q}") for q in range(4)] + [B(f"hh{par}_{q}") for q in range(4)] + [B(f"hpad{par}")]
            hpst = hb[par][:, :, :].ap[0][0]
            HL = HH + T + 2
            RUN = 4 * HL - 4
            for ib in range(4):
                for j in range(4):
                    dst = bass.AP(hs, 32 * j * pst + ib * 4 * HL, [[pst, 32], [1, RUN]])
                    src = bass.AP(hb[par], 32 * ib * hpst + j, [[hpst, 32], [1, RUN]])
                    last_tok = S.emit("sp", (lambda dst=dst, src=src: (lambda e: e.dma_start(out=dst, in_=src)))(),
                                      reads=rd, writes=[B(f"hs_{ib}_{j}")], dma_sem=sem_hs)
            for ib in range(4):
                for j in range(4):
                    B(f"hs_{ib}_{j}").writer = last_tok

        def stage_Epe(i):
            fns = []
            for m in range(NGRP):
                for blk in range(16):
                    q, cj = blk // 4, blk % 4
                    fns.append((lambda m=m, blk=blk, q=q, cj=cj: (lambda e: e.matmul(
                        out=banks[q][32 * cj:32 * cj + 32, 0:T],
                        lhsT=Wd[:, blk * NGRP + m, :],
                        rhs=hs[:, cj * 4 + q, 4 * m:4 * m + T],
                        start=(m == 0), stop=(m == NGRP - 1), tile_position=(0, 32 * cj))))())
            rd = [B("Wd")] + [B(f"hs_{ib}_{j}") for ib in range(4) for j in range(4)]
            S.emit("pe", fns, reads=rd, writes=[bankB[b] for b in range(4)])

        def stage_Eevac(i):
            for b in range(4):
                S.emit("act", (lambda b=b: (lambda e: e.activation(out=cbf[:, b, :], in_=banks[b][:, 0:T], func=AF.Identity,
                                                                  bias=cols[:, C_CB + b:C_CB + b + 1])))(),
                       reads=[bankB[b], B("cols")], writes=[B(f"cbf{b}")])
            for b in range(4):
                S.emit("act", (lambda b=b: (lambda e: e.activation(out=csq[:, b, :], in_=banks[b][:, 0:T], func=AF.Square,
                                                                  bias=cols[:, C_CB + b:C_CB + b + 1])))(),
                       reads=[bankB[b], B("cols")], writes=[B(f"csq{b}")])

        def stage_Eevac32(i):
            for b in range(4):
                S.emit("act", (lambda b=b: (lambda e: e.activation(out=c32[:, b, :], in_=banks[b][:, 0:T], func=AF.Identity,
                                                                  bias=cols[:, C_CB + b:C_CB + b + 1])))(),
                       reads=[bankB[b], B("cols")], writes=[B(f"c32_{b}")])

        def stage_F1(i):
            bk_mu = next_bank()
            bk_sq = next_bank()
            fns = [(lambda iq=iq: (lambda e: e.matmul(out=banks[bk_mu][:, 0:T], lhsT=ones_bf[:, :], rhs=cbf[:, iq, :], start=(iq == 0), stop=(iq == 3))))()
                   for iq in range(4)]
            S.emit("pe", fns, reads=[B("ones_bf")] + [B(f"cbf{iq}") for iq in range(4)], writes=[bankB[bk_mu]])
            fns = [(lambda iq=iq: (lambda e: e.matmul(out=banks[bk_sq][:, 0:T], lhsT=ones_bf[:, :], rhs=csq[:, iq, :], start=(iq == 0), stop=(iq == 3))))()
                   for iq in range(4)]
            S.emit("pe", fns, reads=[B("ones_bf")] + [B(f"csq{iq}") for iq in range(4)], writes=[bankB[bk_sq]])
            S.emit("act", lambda e: e.activation(out=mu_sb[:, :], in_=banks[bk_mu][:, 0:T], func=AF.Copy),
                   reads=[bankB[bk_mu]], writes=[B("mu_sb")])
            S.emit("act", lambda e: e.activation(out=t1[:, :], in_=banks[bk_mu][:, 0:T], func=AF.Square),
                   reads=[bankB[bk_mu]], writes=[B("t1")])
            S.emit("dve", lambda e: e.scalar_tensor_tensor(out=vareps[:, :], in0=banks[bk_sq][:, 0:T], scalar=LN_EPS, in1=t1[:, :],
                                                           op0=ALU.add, op1=ALU.subtract),
                   reads=[bankB[bk_sq], B("t1")], writes=[B("vareps")])
            S.emit("act", lambda e: e.activation(out=nw_a[:, :], in_=vareps[:, :], func=AF.Sqrt),
                   reads=[B("vareps")], writes=[B("nw_a")])

        def stage_F1b(i):
            S.emit("dve", lambda e: e.reciprocal(out=rstd_ln[:, :], in_=nw_a[:, :]),
                   reads=[B("nw_a")], writes=[B("rstd_ln")])
            S.emit("dve", lambda e: e.scalar_tensor_tensor(out=nmr[:, :], in0=mu_sb[:, :], scalar=-1.0, in1=rstd_ln[:, :],
                                                           op0=ALU.mult, op1=ALU.mult),
                   reads=[B("mu_sb"), B("rstd_ln")], writes=[B("nmr")])

        def stage_F2dve(i, pair):
            cp = c32[:, 2 * pair:2 * pair + 2, :]
            pst = list(rstd_ln[:, :].ap[0])
            r_b = bass.AP(rstd_ln, 0, [pst, [0, 2], [1, T]])
            n_b = bass.AP(nmr, 0, [list(nmr[:, :].ap[0]), [0, 2], [1, T]])
            bufs = [B(f"c32_{2 * pair}"), B(f"c32_{2 * pair + 1}")]
            S.emit("dve", lambda e: e.tensor_tensor(out=cp, in0=cp, in1=r_b, op=ALU.mult), reads=bufs + [B("rstd_ln")], writes=bufs)
            S.emit("dve", lambda e: e.tensor_tensor(out=cp, in0=cp, in1=n_b, op=ALU.add), reads=bufs + [B("nmr")], writes=bufs)

        def stage_F2act(i, iqs):
            for iq in iqs:
                S.emit("act", (lambda iq=iq: (lambda e: e.activation(out=s_bf[:, iq, :], in_=c32[:, iq, :], func=AF.Silu,
                                                                    scale=cols[:, C_LG + iq:C_LG + iq + 1], bias=cols[:, C_LB + iq:C_LB + iq + 1])))(),
                       reads=[B(f"c32_{iq}"), B("cols")], writes=[B(f"s_bf{iq}")])

        nw_a = sb("nw_a", [128, T], F32)
        def stage_G(i, fixed_banks=None):
            par = i % 2
            for o in range(4):
                bk = next_bank() if fixed_banks is None else fixed_banks[o]
                fns = [(lambda iq=iq, o=o, bk=bk: (lambda e: e.matmul(out=banks[bk][:, 0:T], lhsT=pw_bf[:, iq, o * 128:(o + 1) * 128], rhs=s_bf[:, iq, :],
                                                                     start=(iq == 0), stop=(iq == 3))))() for iq in range(4)]
                S.emit("pe", fns, reads=[B("pww")] + [B(f"s_bf{iq}") for iq in range(4)], writes=[bankB[bk]])
                S.emit("dve", (lambda o=o, bk=bk: (lambda e: e.scalar_tensor_tensor(out=yT[par][:, 4 + o, :], in0=banks[bk][:, 0:T],
                                                                                  scalar=cols[:, C_PWB + o:C_PWB + o + 1], in1=gb[par][:, o, :],
                                                                                  op0=ALU.add, op1=ALU.mult)))(),
                       reads=[bankB[bk], B(f"gb{par}_{o}"), B("cols")], writes=[B(f"yT{par}_{4 + o}")])

        h1_banks = {}

        def stage_H1(i, phase="both", fixed_banks=None):
            par = i % 2
            slot = i % NXB
            for j in range(2):
                for hf in range(2):
                    if phase in ("both", "first"):
                        bk = next_bank() if fixed_banks is None else fixed_banks[2 * j + hf]
                        h1_banks[(i, j, hf)] = bk
                    else:
                        bk = h1_banks[(i, j, hf)]
                    fns = [(lambda e_=e_, j=j, hf=hf, bk=bk: (lambda e: e.matmul(out=banks[bk][:, :], lhsT=yT[par][:, e_, j * 128:(j + 1) * 128],
                                                                                rhs=w_out_bf[:, e_, hf * 512:(hf + 1) * 512],
                                                                                start=(e_ == 0), stop=(e_ == 7))))() for e_ in range(8)]
                    if phase in ("both", "first"):
                        S.emit("pe", fns[:4], reads=[B("wout")] + [B(f"yT{par}_{e_}") for e_ in range(4)], writes=[bankB[bk]])
                    if phase in ("both", "second"):
                        S.emit("pe", fns[4:], reads=[B("wout")] + [B(f"yT{par}_{e_}") for e_ in range(4, 8)], writes=[bankB[bk]])
                        S.emit("dve", (lambda j=j, hf=hf, bk=bk: (lambda e: e.tensor_tensor(out=xb[slot][:, j, hf * 512:(hf + 1) * 512],
                                                                                           in0=banks[bk][:, :], in1=xb[slot][:, j, hf * 512:(hf + 1) * 512],
                                                                                           op=ALU.add)))(),
                               reads=[bankB[bk], B(f"xb{slot}_{j}_{hf}")], writes=[B(f"xb{slot}_{j}_{hf}")])

        def stage_H2(i):
            slot = i % NXB
            for j in range(2):
                S.emit("act", (lambda j=j: (lambda e: e.activation(out=junk[:, :], in_=xb[slot][:, j, :], func=AF.Square,
                                                                  accum_out=ssq2[:, 2 * i + j:2 * i + j + 1])))(),
                       reads=[B(f"xb{slot}_{j}_0"), B(f"xb{slot}_{j}_1")], writes=[B("junk"), B(f"ssq2{2 * i}")])
            rsqrt_small(ssq2, ms2, rstd2, 2 * i, RMS_EPS, 1.0 / D, "ssq2", "ms2", "rstd2")
            for j in range(2):
                S.emit("dve", (lambda j=j: (lambda e: e.scalar_tensor_tensor(out=xb[slot][:, j, :], in0=xb[slot][:, j, :],
                                                                            scalar=rstd2[:, 2 * i + j:2 * i + j + 1], in1=gfin[:, :],
                                                                            op0=ALU.mult, op1=ALU.mult)))(),
                       reads=[B(f"xb{slot}_{j}_0"), B(f"xb{slot}_{j}_1"), B(f"rstd2{2 * i}"), B("gfin")],
                       writes=[B(f"xb{slot}_{j}_0"), B(f"xb{slot}_{j}_1")])
            return S.emit("sp", lambda e: e.dma_start(out=out_t[i], in_=xb[slot][:, :, :]), reads=xbufs(slot), dma_sem=sem_xs[slot])

        def stage_H2_split(i):
            slot = i % NXB
            toks = []
            for j in range(2):
                c = 2 * i + j
                S.emit("act", (lambda j=j, c=c: (lambda e: e.activation(out=junk[:, :], in_=xb[slot][:, j, :], func=AF.Square,
                                                                       accum_out=ssq2[:, c:c + 1])))(),
                       reads=[B(f"xb{slot}_{j}_0"), B(f"xb{slot}_{j}_1")], writes=[B("junk"), B(f"ssq2s{c}")])
            for j in range(2):
                c = 2 * i + j
                S.emit("pool", (lambda c=c: (lambda e: e.tensor_scalar(out=ms2[:, c:c + 1], in0=ssq2[:, c:c + 1], scalar1=1.0 / D, scalar2=RMS_EPS,
                                                                      op0=ALU.mult, op1=ALU.add)))(),
                       reads=[B(f"ssq2s{c}")], writes=[B(f"ms2s{c}")])
                S.emit("pool", (lambda c=c: (lambda e: e.tensor_tensor(out=rstd2[:, c:c + 1], in0=ms2[:, c:c + 1], in1=neghalf[:, 0:1], op=ALU.pow)))(),
                       reads=[B(f"ms2s{c}"), B("neghalf")], writes=[B(f"rstd2s{c}")])
                if j == 0:
                    S.emit("dve", (lambda j=j, c=c: (lambda e: e.scalar_tensor_tensor(out=xb[slot][:, j, :], in0=xb[slot][:, j, :],
                                                                                     scalar=rstd2[:, c:c + 1], in1=gfin[:, :],
                                                                                     op0=ALU.mult, op1=ALU.mult)))(),
                           reads=[B(f"xb{slot}_{j}_0"), B(f"xb{slot}_{j}_1"), B(f"rstd2s{c}"), B("gfin")],
                           writes=[B(f"xb{slot}_{j}_0"), B(f"xb{slot}_{j}_1")])
                    toks.append(S.emit("sp", (lambda j=j: (lambda e: e.dma_start(out=out_t[i][:, j, :], in_=xb[slot][:, j, :])))(),
                                       reads=[B(f"xb{slot}_{j}_0"), B(f"xb{slot}_{j}_1")], dma_sem=sem_xs[slot]))
                else:
                    for hf in range(2):
                        cs_ = slice(hf * 512, (hf + 1) * 512)
                        S.emit("dve", (lambda j=j, c=c, cs_=cs_: (lambda e: e.scalar_tensor_tensor(out=xb[slot][:, j, cs_], in0=xb[slot][:, j, cs_],
                                                                                                  scalar=rstd2[:, c:c + 1], in1=gfin[:, cs_],
                                                                                                  op0=ALU.mult, op1=ALU.mult)))(),
                               reads=[B(f"xb{slot}_{j}_{hf}"), B(f"rstd2s{c}"), B("gfin")], writes=[B(f"xb{slot}_{j}_{hf}")])
                        toks.append(S.emit("sp", (lambda j=j, cs_=cs_: (lambda e: e.dma_start(out=out_t[i][:, j, cs_], in_=xb[slot][:, j, cs_])))(),
                                           reads=[B(f"xb{slot}_{j}_{hf}")], dma_sem=sem_xs[slot]))
            return toks[-1]

        store_toks = []
        ntiles = NT if debug is None else debug.get("_ntiles", NT)
        def dumps(i):
            par = i % 2
            dbg("xT", xT[par][:, :, :], [128, 8, T], BF16, [B(f"xT{par}_0"), B(f"xT{par}_1")])
            dbg("ub", ub[par][:, :, :], [128, 4, UH + T + 1], F32, [B(f"um{par}_{g}") for g in range(4)])
            dbg("hb", hb[par][:, :, :], [128, 4, HH + T + 2], BF16, [B(f"hm{par}_{g}") for g in range(4)])
            dbg("ga", ga[:, :, :], [128, 4, T], BF16, [B(f"ga{g}") for g in range(4)])
            dbg("gb", gb[par][:, :, :], [128, 4, T], BF16, [B(f"gb{par}_{g}") for g in range(4)])
            dbg("sbf", sbf[:, :, :], [128, 4, T], BF16, [B(f"sbf{g}") for g in range(4)])
            dbg("cbf", cbf[:, :, :], [128, 4, T], BF16, [B(f"cbf{g}") for g in range(4)])
            dbg("rstd_ln", rstd_ln[:, :], [128, T], F32, [B("rstd_ln")])
            dbg("mu_sb", mu_sb[:, :], [128, T], F32, [B("mu_sb")])
            dbg("s_bf", s_bf[:, :, :], [128, 4, T], BF16, [B(f"s_bf{g}") for g in range(4)])
            dbg("yT", yT[par][:, :, :], [128, 8, T], BF16, [B(f"yT{par}_{g}") for g in range(8)])
            dbg("rstdx", rstdx[:, :], [128, 2 * NT], F32, [B(f"rstdx{2 * i}")])

        def tile0_pre():
            stage_Aload(0)
            stage_Acomp(0)
        emit_win("A", hook3=tile0_pre)
        stage_B(0)
        emit_win("B")
        if ntiles > 1:
            stage_Aload(1)
        stage_C(0, ("glu",))
        emit_win("C_dma")
        wide_stage[0] = False
        stage_i[0] = 0
        if ntiles > 1:
            stage_Acomp_a(1)
        stage_C(0, ("bval",))
        stage_stack(0)
        if ntiles > 1:
            stage_Acomp_b(1)
        emit_win("C_cast")
        emit_poolpw_dma(3, 4)
        emit_wout_dma(0, 2)
        emit_gfin()
        stage_C(0, ("aval",))
        stage_Dpool(0)
        stage_C(0, ("agate",))
        stage_Dmm(0, main=False)
        emit_poolpw_cast(3, 4)
        hs_done = [B(f"hs_{ib}_{j}") for ib in range(4) for j in range(4)]
        emit_wout_dma(1, 3, after=hs_done)
        emit_wout_dma(2, 4, after=hs_done)
        stage_C(0, ("bgate",))
        if ntiles > 1:
            stage_B(1)
        emit_wout_cast(0)
        if ntiles > 2:
            stage_Aload(2)
        stage_Epe(0)
        stage_Eevac(0)
        stage_Dmm(0, pre=False)
        if ntiles == 1:
            emit_wout_cast(1)
            emit_wout_cast(2)
            emit_wout_dma(3, 3)
            emit_wout_cast(3)
        for i in range(ntiles):
            n1 = i + 1 < ntiles
            n2 = i + 2 < ntiles
            if n1:
                stage_C(i + 1, ("glu",))
            if i == 0 and ntiles > 1:
                emit_wout_cast(1, eng="dve")
                emit_wout_dma(3, 3)
            stage_F1(i)
            stage_Eevac32(i)
            if n2:
                stage_Acomp_a(i + 2)
            if n1:
                stage_C(i + 1, ("bval",))
                stage_stack(i + 1)
            stage_F1b(i)
            if i == 0 and ntiles > 1:
                emit_wout_cast(2, eng="dve")
                emit_wout_cast(3, eng="dve")
            if n2:
                stage_Acomp_b(i + 2)
            if n1:
                stage_C(i + 1, ("aval",))
                stage_Dpool(i + 1)
            if n1:
                stage_C(i + 1, ("agate",))
            stage_F2dve(i, 0)
            stage_F2dve(i, 1)
            stage_F2act(i, (0, 1))
            if n1:
                stage_C(i + 1, ("bgate",), qs=(0, 1))
            stage_F2act(i, (2,))
            if n1:
                stage_C(i + 1, ("bgate",), qs=(2, 3))
            stage_F2act(i, (3,))
            if n2:
                stage_B(i + 2)
            last_fill = (ntiles >= 3 and i == ntiles - 1)
            if last_fill:
                stage_H1(i - 1)
                stage_H1(i, phase="first", fixed_banks=(0, 1, 2, 3))
                stage_G(i)
            else:
                stage_G(i)
            if i > 0:
                store_toks.append(stage_H2(i - 1))
            if i + 3 < ntiles:
                stage_Aload(i + 3)
            defer_h1 = (ntiles >= 3 and i == ntiles - 2)
            if last_fill:
                stage_H1(i, phase="second")
            elif not defer_h1:
                stage_H1(i)
            if n1:
                stage_Epe(i + 1)
                stage_Eevac(i + 1)
                stage_Dmm(i + 1)
            if debug is not None and i == debug.get("_tile", 0):
                dumps(i)
        store_toks.append(stage_H2_split(ntiles - 1))
        final = {}
        for sem, val in store_toks:
            final[sem] = max(final.get(sem, 0), val)
        if sem_dbg.count:
            final[sem_dbg] = sem_dbg.count
        S.wait_only("sp", list(final.items()))

        with nc.Block() as block:
            @block.sync
            def _(sync):
                S.replay("sp", sync)

            @block.tensor
            def _(tensor):
                S.replay("pe", tensor)

            @block.scalar
            def _(scalar):
                S.replay("act", scalar)

            @block.vector
            def _(vector):
                S.replay("dve", vector)

            @block.gpsimd
            def _(gpsimd):
                S.replay("pool", gpsimd)

    return nc, dbg_out


def _perm_cols(v):
    return np.ascontiguousarray(v.reshape(4, 128).T)


def prepare_inputs(x, norm_g, w_in, pool_w, pool_b, pool_scale, conv_dw, conv_b,
                   ln_g, ln_b, pw_w, pw_b, w_out, final_g):
    f = lambda a: np.ascontiguousarray(np.asarray(a, dtype=np.float32))
    x = f(x)
    cols = np.zeros((128, NCOLS), np.float32)
    cols[:, C_NG:C_NG + 8] = f(norm_g).reshape(8, 128).T
    cols[:, C_PS:C_PS + 4] = f(pool_scale).reshape(4, 128).T
    cols[:, C_PB:C_PB + 4] = f(pool_b).reshape(4, 128).T
    cols[:, C_CB:C_CB + 4] = _perm_cols(f(conv_b))
    cols[:, C_LG:C_LG + 4] = _perm_cols(f(ln_g))
    cols[:, C_LB:C_LB + 4] = _perm_cols(f(ln_b))
    cols[:, C_PWB:C_PWB + 4] = f(pw_b).reshape(4, 128).T
    cw4 = np.zeros((NGRP * 4, 512), np.float32)
    cw4[:KW] = f(conv_dw)
    cols[:, C_CW:] = cw4.reshape(NGRP, 4, 16, 32).transpose(1, 3, 2, 0).reshape(128, 16 * NGRP)
    pww = f(pw_w).reshape(4, 128, 512).transpose(1, 0, 2).reshape(128, 4 * 512)
    plw = f(pool_w).transpose(1, 0, 2).reshape(128, 4 * 128)
    gfin = np.broadcast_to(f(final_g)[None, :], (128, D))
    shared = {
        "w_in": f(w_in), "w_out": f(w_out), "pw_w": np.ascontiguousarray(pww),
        "pool_w": np.ascontiguousarray(plw), "cols": cols, "gfin": np.ascontiguousarray(gfin),
    }
    xs = x.reshape(NCORES, TOK_PER_CORE, D)
    in_maps = []
    for c in range(NCORES):
        m = dict(shared)
        m["x"] = np.ascontiguousarray(xs[c])
        in_maps.append(m)
    return in_maps


def kernel(x, norm_g, w_in, pool_w, pool_b, pool_scale, conv_dw, conv_b,
           ln_g, ln_b, pw_w, pw_b, w_out, final_g):
    in_maps = prepare_inputs(x, norm_g, w_in, pool_w, pool_b, pool_scale, conv_dw, conv_b,
                             ln_g, ln_b, pw_w, pw_b, w_out, final_g)
    nc, _ = build_nc()
    res = run_bass_kernel_spmd(nc, in_maps, core_ids=list(range(NCORES)))
    out = np.stack([np.asarray(r["out"], dtype=np.float32) for r in res.results], axis=0)
    return out.reshape(16, SEQ, D)
```
